# Optimizing a Trainium2 kernel written in Bass

```python
import math
import jax, jax.numpy as jnp
from jax import lax
import numpy as np

D_MODEL = 2048
BATCH = 4
SEQ = 2048
DEPTH = 4

CHUNK = 64
N_MIXERS = 3
N_HEADS = 16
HEAD_DIM = D_MODEL // N_HEADS
Q_BLOCK = 128
LEFT_CHUNKS = 8
BAND = LEFT_CHUNKS + 1
REL_CLIP = 128
CONV_K = 31
D_FF = ((8 * D_MODEL // 3 + 255) // 256) * 256
ALPHA = (2.0 * DEPTH) ** 0.25
BETA = (8.0 * DEPTH) ** -0.25
LN_EPS = 1e-5
N_A = (DEPTH + 2) // 3
N_B = (DEPTH + 1) // 3
N_C = DEPTH // 3

kernel_name = "hybrid_fox_chunkrel_conformer_deepnorm"


def layer_norm(x, g, b):
    xf = x.astype(jnp.float32)
    mu = jnp.mean(xf, axis=-1, keepdims=True)
    var = jnp.mean(jnp.square(xf - mu), axis=-1, keepdims=True)
    y = (xf - mu) * lax.rsqrt(var + LN_EPS) * g.astype(jnp.float32) + b.astype(jnp.float32)
    return y.astype(x.dtype)


def split_heads(qkv):
    B, S, _ = qkv.shape
    q, k, v = jnp.split(qkv, 3, axis=-1)
    shp = (B, S, N_HEADS, HEAD_DIM)
    return q.reshape(shp), k.reshape(shp), v.reshape(shp)


def fox_attention(h, w_qkv, w_f, b_f, w_o):
    B, S, _ = h.shape
    q, k, v = split_heads(h @ w_qkv)
    log_f = jax.nn.log_sigmoid((h @ w_f + b_f).astype(jnp.float32))
    c = jnp.transpose(jnp.cumsum(log_f, axis=1), (0, 2, 1))
    scale = HEAD_DIM ** -0.5
    outs = []
    for blk in range(S // Q_BLOCK):
        q0, q1 = blk * Q_BLOCK, (blk + 1) * Q_BLOCK
        s = jnp.einsum('bqhd,bkhd->bhqk', q[:, q0:q1], k[:, :q1]).astype(jnp.float32) * scale
        s = s + (c[:, :, q0:q1, None] - c[:, :, None, :q1])
        causal = (q0 + jnp.arange(Q_BLOCK))[:, None] >= jnp.arange(q1)[None, :]
        p = jax.nn.softmax(jnp.where(causal[None, None], s, -jnp.inf), axis=-1)
        outs.append(jnp.einsum('bhqk,bkhd->bqhd', p.astype(v.dtype), v[:, :q1]))
    o = jnp.concatenate(outs, axis=1).reshape(B, S, D_MODEL)
    return o @ w_o


def chunk_relpos_attention(h, w_qkv, rel_bias, w_o):
    B, S, _ = h.shape
    nc = S // CHUNK
    q, k, v = split_heads(h @ w_qkv)
    pad = LEFT_CHUNKS * CHUNK
    qc = q.reshape(B, nc, CHUNK, N_HEADS, HEAD_DIM)
    padw = ((0, 0), (pad, 0), (0, 0), (0, 0))
    k_pad = jnp.pad(k, padw).reshape(B, nc + LEFT_CHUNKS, CHUNK, N_HEADS, HEAD_DIM)
    v_pad = jnp.pad(v, padw).reshape(B, nc + LEFT_CHUNKS, CHUNK, N_HEADS, HEAD_DIM)
    band_idx = jnp.arange(nc)[:, None] + jnp.arange(BAND)[None, :]
    kb = k_pad[:, band_idx].reshape(B, nc, BAND * CHUNK, N_HEADS, HEAD_DIM)
    vb = v_pad[:, band_idx].reshape(B, nc, BAND * CHUNK, N_HEADS, HEAD_DIM)
    s = jnp.einsum('bnqhd,bnkhd->bnhqk', qc, kb).astype(jnp.float32) * (HEAD_DIM ** -0.5)
    k_off = jnp.arange(BAND * CHUNK)
    rel = pad + jnp.arange(CHUNK)[:, None] - k_off[None, :]
    bias = rel_bias[:, jnp.clip(rel, -REL_CLIP, REL_CLIP) + REL_CLIP]
    s = s + bias[None, None].astype(jnp.float32)
    valid = (jnp.arange(nc)[:, None] * CHUNK - pad + k_off[None, :]) >= 0
    p = jax.nn.softmax(jnp.where(valid[None, :, None, None, :], s, -jnp.inf), axis=-1)
    o = jnp.einsum('bnhqk,bnkhd->bnqhd', p.astype(vb.dtype), vb).reshape(B, S, D_MODEL)
    return o @ w_o


def conformer_conv(h, w_pw1, b_pw1, w_dw, b_dw, ln_g, ln_b, w_pw2, b_pw2):
    u = h @ w_pw1 + b_pw1
    a, g = jnp.split(u, 2, axis=-1)
    u = a * jax.nn.sigmoid(g)
    y = lax.conv_general_dilated(
        u, w_dw[:, None, :].astype(u.dtype), window_strides=(1,),
        padding=[(CONV_K - 1, 0)], dimension_numbers=('NWC', 'WIO', 'NWC'),
        feature_group_count=D_MODEL) + b_dw
    y = jax.nn.silu(layer_norm(y, ln_g, ln_b))
    return y @ w_pw2 + b_pw2


def swiglu_ffn(h, w_gate, w_up, w_down):
    return (jax.nn.silu(h @ w_gate) * (h @ w_up)) @ w_down


def setup_inputs(seed: int = 0) -> dict:
    key = jax.random.key(seed)
    ks = iter(jax.random.split(key, 32))
    f32 = jnp.float32

    def nrm(shape, scale):
        return jax.random.normal(next(ks), shape, f32) * scale

    D, F, H = D_MODEL, D_FF, N_HEADS
    s_d = D ** -0.5

    def qkv_w(n):
        qk = nrm((n, D, 2 * D), s_d)
        v = nrm((n, D, D), s_d * BETA)
        return jnp.concatenate([qk, v], axis=-1)

    return {
        "x": jax.random.normal(next(ks), (BATCH, SEQ, D), f32),
        "fox_w_qkv": qkv_w(N_A),
        "fox_w_f": nrm((N_A, D, H), s_d),
        "fox_b_f": 3.0 + nrm((N_A, H), 0.5),
        "fox_w_o": nrm((N_A, D, D), s_d * BETA),
        "rel_w_qkv": qkv_w(N_B),
        "rel_bias": nrm((N_B, H, 2 * REL_CLIP + 1), 0.2),
        "rel_w_o": nrm((N_B, D, D), s_d * BETA),
        "conv_w_pw1": nrm((N_C, D, 2 * D), s_d),
        "conv_b_pw1": nrm((N_C, 2 * D), 0.02),
        "conv_w_dw": nrm((N_C, CONV_K, D), CONV_K ** -0.5),
        "conv_b_dw": nrm((N_C, D), 0.02),
        "conv_ln_g": 1.0 + nrm((N_C, D), 0.02),
        "conv_ln_b": nrm((N_C, D), 0.02),
        "conv_w_pw2": nrm((N_C, D, D), s_d * BETA),
        "conv_b_pw2": nrm((N_C, D), 0.02),
        "ffn_w_gate": nrm((DEPTH, D, F), s_d),
        "ffn_w_up": nrm((DEPTH, D, F), s_d),
        "ffn_w_down": nrm((DEPTH, F, D), F ** -0.5 * BETA),
        "ln_mix_g": 1.0 + nrm((DEPTH, D), 0.02),
        "ln_mix_b": nrm((DEPTH, D), 0.02),
        "ln_ffn_g": 1.0 + nrm((DEPTH, D), 0.02),
        "ln_ffn_b": nrm((DEPTH, D), 0.02),
    }


def reference(x, fox_w_qkv, fox_w_f, fox_b_f, fox_w_o,
              rel_w_qkv, rel_bias, rel_w_o,
              conv_w_pw1, conv_b_pw1, conv_w_dw, conv_b_dw, conv_ln_g, conv_ln_b, conv_w_pw2, conv_b_pw2,
              ffn_w_gate, ffn_w_up, ffn_w_down,
              ln_mix_g, ln_mix_b, ln_ffn_g, ln_ffn_b):
    for i in range(DEPTH):
        kind, j = i % N_MIXERS, i // N_MIXERS
        if kind == 0:
            m = fox_attention(x, fox_w_qkv[j], fox_w_f[j], fox_b_f[j], fox_w_o[j])
        elif kind == 1:
            m = chunk_relpos_attention(x, rel_w_qkv[j], rel_bias[j], rel_w_o[j])
        else:
            m = conformer_conv(x, conv_w_pw1[j], conv_b_pw1[j], conv_w_dw[j], conv_b_dw[j],
                               conv_ln_g[j], conv_ln_b[j], conv_w_pw2[j], conv_b_pw2[j])
        x = layer_norm(ALPHA * x + m, ln_mix_g[i], ln_mix_b[i])
        f = swiglu_ffn(x, ffn_w_gate[i], ffn_w_up[i], ffn_w_down[i])
        x = layer_norm(ALPHA * x + f, ln_ffn_g[i], ln_ffn_b[i])
    return x
```

```python
import contextlib
import math
import numpy as np
import concourse.bass as bass
import concourse.mybir as mybir
from concourse.bass_utils import run_bass_kernel_spmd

F32 = mybir.dt.float32
BF16 = mybir.dt.bfloat16
AF = mybir.ActivationFunctionType
ALU = mybir.AluOpType
AX = mybir.AxisListType

D = 2048
T = 1024
FC = 16
H = 16
F = 5632
NG = F // 512
DEPTH = 4
ALPHA = (2.0 * DEPTH) ** 0.25
LN_EPS = 1e-5
QSCALE = 128 ** -0.5
NEG = -30000.0
NBLK = 14
NCORES = 8
GROUPS = [[0, 1], [2, 3], [4, 5], [6, 7]]
DEBUG = {}

PP_LN = 0
PP_BF = 256
PP_CTXMASK = 258
PP_HASPREV = 259
NPP = 260
CP_BPW1 = 0
CP_WDW = 32
CP_BDW = 528
CP_LNG = 544
CP_LNB = 560
CP_BPW2 = 576
NCP = 592
CS_ONES = 0
CS_TRI = 128
CS_ID = 256
CS_SEL = 272
CS_IDB = 272 + 2048
NCS = 272 + 2048 + 128


class Buf:
    __slots__ = ("name", "w", "rs")

    def __init__(self, name):
        self.name = name
        self.w = None
        self.rs = {}


class Op:
    __slots__ = ("waits", "fn", "inc")

    def __init__(self, waits, fn, inc):
        self.waits, self.fn, self.inc = waits, fn, inc


ENG = ["pe", "act", "dve", "pool", "sp"]


class Sched:
    def __init__(self, nc, stack):
        self.nc, self.stack = nc, stack
        self.ops = {e: [] for e in ENG}
        self.sems = {}
        self.count = {}
        self.eng_key = {}
        self.eng_n = {e: 0 for e in ENG}
        self.seen = {e: {} for e in ENG}
        self.dry = False
        self.final = []

    def _sem(self, key):
        if key not in self.sems:
            self.sems[key] = self.stack.enter_context(self.nc.semaphore(key))
            self.count[key] = 0
        return key

    def _waits(self, eng, reads, writes, tok_sem, is_dma):
        need = {}

        def add(t, kind):
            if t is None:
                return
            sem, val, teng = t
            if (not is_dma) and teng == eng:
                if eng == "pe":
                    return
            if is_dma and sem == tok_sem and kind == "waw":
                return
            if need.get(sem, 0) < val:
                need[sem] = val

        for b in reads:
            add(b.w, "raw")
        for b in writes:
            add(b.w, "waw")
            for sem, (val, teng) in b.rs.items():
                add((sem, val, teng), "war")
        waits = []
        for sem, val in need.items():
            if self.seen[eng].get(sem, 0) < val:
                self.seen[eng][sem] = val
                waits.append((sem, val))
        return waits

    def _update(self, tok, reads, writes):
        for b in reads:
            b.rs[tok[0]] = (tok[1], tok[2])
        for b in writes:
            b.w = tok
            b.rs = {}

    def op(self, eng, fn, reads=(), writes=()):
        if self.dry:
            return
        k = self.eng_key.get(eng)
        if k is None or self.count[k] >= 3000:
            k = f"{eng}_{self.eng_n[eng]}"
            self.eng_n[eng] += 1
            self._sem(k)
            self.eng_key[eng] = k
        waits = self._waits(eng, reads, writes, k, False)
        self.count[k] += 1
        tok = (k, self.count[k], eng)
        self.ops[eng].append(Op(waits, fn, (k, 1)))
        self._update(tok, reads, writes)
        return tok

    def dma(self, queue, chan, fn, reads=(), writes=(), inc=16):
        if self.dry:
            return
        k = self._sem("d_" + chan)
        waits = self._waits(queue, reads, writes, k, True)
        self.count[k] += inc
        tok = (k, self.count[k], "dma")
        self.ops[queue].append(Op(waits, fn, (k, inc)))
        self._update(tok, reads, writes)
        return tok

    def check_deadlock(self):
        cnt = {k: 0 for k in self.sems}
        ptr = {e: 0 for e in ENG}
        progress = True
        while progress:
            progress = False
            for e in ENG:
                while ptr[e] < len(self.ops[e]):
                    o = self.ops[e][ptr[e]]
                    if all(cnt[s] >= v for s, v in o.waits):
                        cnt[o.inc[0]] += o.inc[1]
                        ptr[e] += 1
                        progress = True
                    else:
                        break
        stuck = {e: (ptr[e], len(self.ops[e])) for e in ENG if ptr[e] < len(self.ops[e])}
        if stuck:
            msg = []
            for e, (p, n) in stuck.items():
                o = self.ops[e][p]
                msg.append(f"{e} stuck at op {p}/{n} waits={[(s, v, cnt[s]) for s, v in o.waits if cnt[s] < v]}")
            raise RuntimeError("semaphore deadlock: " + "; ".join(msg))
        for s, v in self.final:
            assert cnt[s] >= v

    def emit(self):
        nc = self.nc
        self.check_deadlock()
        with nc.Block() as block:
            def mk(name):
                def f(e):
                    for o in self.ops[name]:
                        for sem, val in o.waits:
                            e.wait_ge(self.sems[sem], val)
                        last = o.fn(e)
                        last.then_inc(self.sems[o.inc[0]], o.inc[1])
                    if name == "sp":
                        for sem, val in self.final:
                            e.wait_ge(self.sems[sem], val)
                return f
            block.tensor(mk("pe"))
            block.scalar(mk("act"))
            block.vector(mk("dve"))
            block.gpsimd(mk("pool"))
            block.sync(mk("sp"))


class WStream:
    def __init__(self, S, ring_tensor, nr):
        self.S = S
        self.ring = ring_tensor
        self.nr = nr
        self.bufs = [Buf(f"ring{i}") for i in range(nr)]
        self.srcs = []
        self.next_get = 0
        self.next_load = 0
        self.after = ()

    def view(self, i, kind):
        r = self.ring[:, i * 8192:(i + 1) * 8192]
        if kind == "A":
            return r.rearrange("p (k n) -> p k n", k=16)
        return r.rearrange("p (k n) -> p k n", k=4)

    def _issue(self, idx):
        src, kind = self.srcs[idx]
        i = idx % self.nr
        dst = self.view(i, kind)

        def fn(e, dst=dst, src=src):
            return e.dma_start(out=dst, in_=src)
        self.S.dma("pool", f"ring{i}", fn, reads=(self.after if idx in (1, 2) else ()), writes=(self.bufs[i],))

    def get(self, src, kind, live=0):
        if self.S.dry:
            self.srcs.append((src, kind))
            return None, None
        idx = self.next_get
        self.next_get += 1
        while self.next_load < min(len(self.srcs), idx + self.nr - live):
            self._issue(self.next_load)
            self.next_load += 1
        i = idx % self.nr
        return self.view(i, kind), self.bufs[i]


def wslabA(w, c0):
    return w.rearrange("(k p) n -> p k n", p=128)[:, :, c0:c0 + 512]


def wslabB(w, r0):
    return w[r0:r0 + 512, :].rearrange("(k p) n -> p k n", p=128)


def build(layers, do_mixer=True, do_ffn=True, final_ln_only=False):
    nc = bass.Bass("TRN2", target_bir_lowering=False)
    dram = {}

    def din(name, shape, dt=F32):
        dram[name] = nc.dram_tensor(name, list(shape), dt, kind="ExternalInput").ap()
        return dram[name]

    xT = din("xT", [D, T])
    ppd = din("pp", [128, NPP])
    cstd = din("cst", [128, NCS])
    outT = nc.dram_tensor("outT", [D, T], F32, kind="ExternalOutput").ap()
    W = {}
    for i in layers:
        kind = i % 3
        if do_mixer:
            if kind in (0, 1):
                W[i, "qkv"] = din(f"wqkv{i}", [D, 3 * D])
                W[i, "o"] = din(f"wo{i}", [D, D])
                if kind == 0:
                    W[i, "wf"] = din(f"wf{i}", [128, 16 * 48])
                else:
                    W[i, "relb"] = din(f"relb{i}", [H, 128, 640])
            else:
                W[i, "pw1"] = din(f"pw1{i}", [D, 2 * D])
                W[i, "pw2"] = din(f"pw2{i}", [D, D])
                W[i, "cp"] = din(f"cp{i}", [128, NCP])
        if do_ffn:
            W[i, "g"] = din(f"wg{i}", [D, F])
            W[i, "u"] = din(f"wu{i}", [D, F])
            W[i, "d"] = din(f"wd{i}", [F, D])

    def dscr(name, shape, dt):
        return nc.dram_tensor(name, list(shape), dt, kind="Internal").ap()

    kT_own = [dscr(f"kT_own{p}", [D // 2, T], BF16) for p in range(2)]
    kT_g = [dscr(f"kT_g{p}", [D, T], BF16) for p in range(2)]
    v_own = [dscr(f"v_own{p}", [T, D // 2], BF16) for p in range(2)]
    v_g = [dscr(f"v_g{p}", [2 * T, D // 2], BF16) for p in range(2)]
    c_own = dscr("c_own", [16, T], F32)
    c_g = dscr("c_g", [32, T], F32)
    halo_own = [dscr(f"halo_own{g}", [512, 32], BF16) for g in range(4)]
    halo_g = [dscr(f"halo_g{g}", [1024, 32], BF16) for g in range(4)]
    yscr = dscr("yscr", [D, T], F32)

    stack = contextlib.ExitStack()
    with stack:
        def sb(name, shape, dt):
            return stack.enter_context(nc.sbuf_tensor("s_" + name, list(shape), dt))
        xr = sb("xr", [128, FC, T], F32)
        xb = sb("xb", [128, FC, T], BF16)
        ring = sb("ring", [128, 3 * 8192], BF16)
        arena = sb("arena", [128, NBLK * 1024], F32)
        pp = sb("pp", [128, NPP], F32)
        ones_f = sb("ones_f", [128, 128], F32)
        ones_b = sb("ones_b", [128, 128], BF16)
        tri_b = sb("tri_b", [128, 128], BF16)
        ident_b = sb("ident_b", [128, 128], BF16)
        ident = sb("ident", [16, 16], F32)
        wf_t = sb("wf_t", [128, 16 * 48], BF16)
        kb = sb("kb", [128, 256], F32)
        rl = sb("rl", [128, 512], F32)
        nbf = sb("nbf", [48, 2], F32)
        ps = stack.enter_context(nc.psum_tensor("psum_all", [128, 8, 512], F32))

        S = Sched(nc, stack)
        ws = WStream(S, ring, 3)
        B_xr = [Buf(f"xr{i}") for i in range(FC)]
        B_xb = [Buf(f"xb{i}") for i in range(FC)]
        B_ps = [Buf(f"ps{i}") for i in range(8)]
        B_ar = [Buf(f"ar{i}") for i in range(NBLK)]
        B_pp, B_cst, B_kb, B_rl, B_wf, B_nbf = Buf("pp"), Buf("cst"), Buf("kb"), Buf("rl"), Buf("wf"), Buf("nbf")
        B_co, B_cg = (Buf(n) for n in ("co", "cg"))
        B_y = [Buf("yscr0"), Buf("yscr1")]
        B_kTg, B_vg = ([Buf(n + str(p)) for p in range(2)] for n in ("kTg", "vg"))
        B_kTo, B_vo = ([[Buf(f"{n}{p}_{j}") for j in range(2)] for p in range(2)] for n in ("kTo", "vo"))
        B_ho = [Buf(f"ho{g}") for g in range(4)]
        B_hg = [Buf(f"hg{g}") for g in range(4)]
        B_out = Buf("out")
        ws.after = (B_xr[FC - 1],)
        B_pT = [Buf(f"pT{i}") for i in range(4)]
        B_lacc = [Buf("lacc0"), Buf("lacc1")]
        B_stg = [Buf("stg0"), Buf("stg1")]
        B_ps5 = [Buf("ps5a"), Buf("ps5b")]
        B_ps7 = [Buf("ps7a"), Buf("ps7b")]
        B_rl2 = [Buf("rl2a"), Buf("rl2b")]

        def alias_sync(subs, blk):
            def absorb(dst, srcb):
                for k_, v_ in srcb.rs.items():
                    if dst.rs.get(k_, (0, None))[0] < v_[0]:
                        dst.rs[k_] = v_
                if srcb.w is not None:
                    k_, val_, eng_ = srcb.w
                    if dst.rs.get(k_, (0, None))[0] < val_:
                        dst.rs[k_] = (val_, eng_)
            for s_ in subs:
                absorb(blk, s_)
            for s_ in subs:
                absorb(s_, blk)

        def alias_sync_all():
            alias_sync(B_stg, B_ar[8])
            alias_sync(B_pT, B_ar[6])
            alias_sync(B_rl2, B_rl)

        def AF32(b0, nb=1):
            return arena[:, b0 * 1024:(b0 + nb) * 1024]

        def ABF(b0, nb=1):
            return arena[:, b0 * 1024:(b0 + nb) * 1024].bitcast(BF16)

        def AB(b0, nb=1):
            return B_ar[b0:b0 + nb]

        def ppc(c, n=1, rows=128):
            return pp[0:rows, c:c + n]

        state = {"pair": 0, "first_acc": True}
        pairs_all = [(0, 1), (2, 3), (4, 5), (6, 7)]
        state["pairs"] = pairs_all

        def next_pair():
            p = state["pairs"][state["pair"] % len(state["pairs"])]
            state["pair"] += 1
            return p

        def ps2(pair):
            return ps[:, pair[0]:pair[0] + 2, :]

        def v2(ap):
            return ap.rearrange("p (a b) -> p a b", a=2)

        def proj_fm(slab, sbuf_, evac, nchs=4):
            if state.get("after_ln") and len(state["pairs"]) == 4:
                state["after_ln"] = False
                banks = pairs_all
                for kc in range(16):
                    def fk(pe, kc=kc):
                        last = None
                        for nch in range(4):
                            for tb in range(2):
                                last = pe.matmul(ps[:, banks[nch][tb], :], lhsT=slab[:, kc, nch * 128:(nch + 1) * 128],
                                                 rhs=xb[:, kc, tb * 512:(tb + 1) * 512], start=(kc == 0), stop=(kc == 15))
                        return last
                    S.op("pe", fk, reads=[sbuf_, B_xb[kc]], writes=B_ps)
                for nch in range(4):
                    evac(nch, banks[nch])
                return
            state["after_ln"] = False
            for nch in range(nchs):
                pair = next_pair()

                def fn(pe, nch=nch, pair=pair):
                    last = None
                    for kc in range(16):
                        for tb in range(2):
                            last = pe.matmul(ps[:, pair[tb], :], lhsT=slab[:, kc, nch * 128:(nch + 1) * 128],
                                             rhs=xb[:, kc, tb * 512:(tb + 1) * 512], start=(kc == 0), stop=(kc == 15))
                    return last
                S.op("pe", fn, reads=[sbuf_] + B_xb, writes=[B_ps[pair[0]], B_ps[pair[1]]])
                evac(nch, pair)

        def proj_acc(src, src_bufs, nk, slab, sbuf_, last=False, bias=None):
            first = state["first_acc"]
            state["first_acc"] = False
            s1, s2, sqb = AF32(12), AF32(11), AF32(10)
            for n in range(16):
                pair = next_pair()

                def fn(pe, n=n, pair=pair):
                    last_i = None
                    for kc in range(nk):
                        for tb in range(2):
                            last_i = pe.matmul(ps[:, pair[tb], :], lhsT=slab[:, kc, n * 128:(n + 1) * 128],
                                               rhs=src[:, kc, tb * 512:(tb + 1) * 512], start=(kc == 0), stop=(kc == nk - 1))
                    return last_i
                S.op("pe", fn, reads=[sbuf_] + list(src_bufs), writes=[B_ps[pair[0]], B_ps[pair[1]]])
                rd = [B_ps[pair[0]], B_ps[pair[1]], B_xr[n]]
                if first:
                    assert bias is None
                    def fa(e, n=n, pair=pair):
                        return e.scalar_tensor_tensor(out=v2(xr[:, n, :]), in0=v2(xr[:, n, :]), scalar=ALPHA,
                                                      in1=ps2(pair), op0=ALU.mult, op1=ALU.add)
                elif bias is not None:
                    bc, bb = bias
                    rd = rd + list(bb)
                    def fa(e, n=n, pair=pair, bc=bc):
                        return e.scalar_tensor_tensor(out=v2(xr[:, n, :]), in0=ps2(pair), scalar=bc(n),
                                                      in1=v2(xr[:, n, :]), op0=ALU.add, op1=ALU.add)
                else:
                    def fa(e, n=n, pair=pair):
                        return e.tensor_tensor(out=v2(xr[:, n, :]), in0=ps2(pair), in1=v2(xr[:, n, :]), op=ALU.add)
                S.op("dve", fa, reads=rd, writes=[B_xr[n]])
                if last:
                    if n == 0:
                        S.op("pool", lambda e, n=n: e.tensor_copy(out=s1, in_=xr[:, n, :]), reads=[B_xr[n]], writes=AB(12))
                    else:
                        S.op("pool", lambda e, n=n: e.tensor_tensor(out=s1, in0=s1, in1=xr[:, n, :], op=ALU.add),
                             reads=[B_xr[n]] + AB(12), writes=AB(12))
                    S.op("act", lambda e, n=n: e.activation(out=sqb, in_=xr[:, n, :], func=AF.Square), reads=[B_xr[n]], writes=AB(10))
                    if n == 0:
                        S.op("dve", lambda e: e.tensor_copy(out=s2, in_=sqb), reads=AB(10), writes=AB(11))
                    else:
                        S.op("dve", lambda e: e.tensor_tensor(out=s2, in0=s2, in1=sqb, op=ALU.add), reads=AB(10) + AB(11), writes=AB(11))

        def ln_stats_finish(s1, s2, b_s1, b_s2, mean, rstd, b_mean, b_rstd, tmp, b_tmp):
            pa, pb = next_pair(), next_pair()

            def f1(pe):
                last = None
                for tb in range(2):
                    last = pe.matmul(ps[:, pa[tb], :], lhsT=ones_f[:, :], rhs=s1[:, tb * 512:(tb + 1) * 512], start=True, stop=True)
                for tb in range(2):
                    last = pe.matmul(ps[:, pb[tb], :], lhsT=ones_f[:, :], rhs=s2[:, tb * 512:(tb + 1) * 512], start=True, stop=True)
                return last
            S.op("pe", f1, reads=[b_s1, b_s2, B_cst], writes=[B_ps[pa[0]], B_ps[pa[1]], B_ps[pb[0]], B_ps[pb[1]]])
            S.op("act", lambda e: e.activation(out=v2(mean), in_=ps2(pa), func=AF.Copy, scale=1.0 / D),
                 reads=[B_ps[pa[0]], B_ps[pa[1]]], writes=[b_mean])
            S.op("dve", lambda e: e.tensor_tensor(out=tmp, in0=mean, in1=mean, op=ALU.mult), reads=[b_mean], writes=[b_tmp])
            S.op("dve", lambda e: e.scalar_tensor_tensor(out=v2(rstd), in0=ps2(pb), scalar=1.0 / D, in1=v2(tmp),
                                                         op0=ALU.mult, op1=ALU.subtract),
                 reads=[B_ps[pb[0]], B_ps[pb[1]], b_tmp], writes=[b_rstd])
            S.op("act", lambda e: e.activation(out=rstd, in_=rstd, func=AF.Sqrt, bias=epsc[:, 0:1], scale=1.0),
                 reads=[b_rstd, B_cst], writes=[b_rstd])
            S.op("dve", lambda e: e.reciprocal(out=rstd, in_=rstd), reads=[b_rstd], writes=[b_rstd])

        def layer_norm(gcol, bcol, final=False):
            alias_sync_all()
            s1, s2, mean, rstd = AF32(12), AF32(11), AF32(0), AF32(1)
            t = [AF32(2), AF32(3)]
            yst = [AF32(4), AF32(5)]
            ln_stats_finish(s1, s2, B_ar[12], B_ar[11], mean, rstd, B_ar[0], B_ar[1], AF32(6), B_ar[6])
            t = [AF32(2), AF32(3), AF32(7), AF32(8)]
            tbuf = [AB(2), AB(3), AB(7), AB(8)]

            def ln_sub(fc):
                tt, bt = t[fc % 4], tbuf[fc % 4]
                S.op("dve", lambda e: e.tensor_tensor(out=tt, in0=xr[:, fc, :], in1=mean, op=ALU.subtract),
                     reads=[B_xr[fc], B_ar[0]], writes=bt)

            def ln_rest(fc):
                tt, bt = t[fc % 4], tbuf[fc % 4]
                S.op("dve", lambda e: e.tensor_tensor(out=tt, in0=tt, in1=rstd, op=ALU.mult), reads=bt + [B_ar[1]], writes=bt)
                if not final:
                    S.op("act", lambda e: e.activation(out=xb[:, fc, :], in_=tt, func=AF.Identity,
                                                       scale=ppc(gcol + fc), bias=ppc(bcol + fc)),
                         reads=bt + [B_pp], writes=[B_xb[fc]])
                    S.op("act", lambda e: e.activation(out=xr[:, fc, :], in_=tt, func=AF.Identity,
                                                       scale=ppc(gcol + fc), bias=ppc(bcol + fc)),
                         reads=bt + [B_pp], writes=[B_xr[fc]])
                else:
                    ys = yst[fc % 2]
                    by = AB(4 + fc % 2)
                    S.op("act", lambda e: e.activation(out=ys, in_=tt, func=AF.Identity,
                                                       scale=ppc(gcol + fc), bias=ppc(bcol + fc)),
                         reads=bt + [B_pp], writes=by)
                    tok = S.dma("sp", f"out{fc % 2}", lambda e: e.dma_start(out=outT[fc * 128:(fc + 1) * 128, :], in_=ys),
                                reads=by, writes=[B_out])
                    if tok is not None:
                        S.final.append((tok[0], tok[1]))
            for f2 in range(0, FC, 2):
                ln_sub(f2)
                ln_sub(f2 + 1)
                ln_rest(f2)
                ln_rest(f2 + 1)
            alias_sync_all()
            state["first_acc"] = True
            state["after_ln"] = True

        def ffn(i):
            hg = ABF(0, 2).rearrange("p (k t) -> p k t", k=4)
            hb = [ABF(2, 2).rearrange("p (k t) -> p k t", k=4), ABF(4, 2).rearrange("p (k t) -> p k t", k=4)]
            hbb = [AB(2, 2), AB(4, 2)]
            for g in range(NG):
                slab, sbuf_ = ws.get(wslabA(W[i, "g"], g * 512), "A")

                def ev_g(nch, pair):
                    S.op("act", lambda e: e.activation(out=v2(hg[:, nch, :]), in_=ps2(pair), func=AF.Silu),
                         reads=[B_ps[pair[0]], B_ps[pair[1]]], writes=AB(0, 2))
                if not S.dry:
                    proj_fm(slab, sbuf_, ev_g)
                slab, sbuf_ = ws.get(wslabA(W[i, "u"], g * 512), "A")
                hcur, hcb = hb[g % 2], hbb[g % 2]

                def ev_u(nch, pair, hcur=hcur, hcb=hcb):
                    S.op("dve", lambda e: e.tensor_tensor(out=v2(hcur[:, nch, :]), in0=ps2(pair), in1=v2(hg[:, nch, :]), op=ALU.mult),
                         reads=[B_ps[pair[0]], B_ps[pair[1]]] + AB(0, 2), writes=hcb)
                if not S.dry:
                    proj_fm(slab, sbuf_, ev_u)
                slab, sbuf_ = ws.get(wslabB(W[i, "d"], g * 512), "B")
                if not S.dry:
                    proj_acc(hcur, hcb, 4, slab, sbuf_, last=(g == NG - 1))

        def attn_phase_a(i, fox):
            wq = W[i, "qkv"]
            stg = [ABF(8)[:, 0:1024], ABF(8)[:, 1024:2048]]
            k = [0]

            def gather(name, p_, own, gat, b_own, b_gat):
                S.dma("pool", f"cc_{name}{p_}", lambda e: e.collective_compute("AllGather", ALU.bypass, replica_groups=GROUPS,
                                                                               ins=[own[:, :]], outs=[gat[:, :]]),
                      reads=b_own, writes=[b_gat], inc=1)

            for s in range(4):
                slab, sbuf_ = ws.get(wslabA(wq, D + s * 512), "A")

                def ev_k(nch, pair, s=s):
                    j = k[0] % 2
                    k[0] += 1
                    S.op("act", lambda e: e.activation(out=v2(stg[j]), in_=ps2(pair), func=AF.Copy),
                         reads=[B_ps[pair[0]], B_ps[pair[1]]], writes=[B_stg[j]])
                    r0 = (s * 4 + nch) * 128
                    hp, rr = r0 // 1024, r0 % 1024
                    S.dma("sp", f"stg{j}", lambda e: e.dma_start(out=kT_own[hp][rr:rr + 128, :], in_=stg[j]),
                          reads=[B_stg[j]], writes=[B_kTo[hp][j]])
                if not S.dry:
                    proj_fm(slab, sbuf_, ev_k)
                    if s % 2 == 1:
                        gather("k", s // 2, kT_own[s // 2], kT_g[s // 2], B_kTo[s // 2], B_kTg[s // 2])
            if fox and not S.dry:
                fox_forget(i)

            def v_tile(s, tt, slab, sbuf_):
                pair = next_pair()
                bank = pair[0]

                def fn(pe):
                    last = None
                    for kc in range(16):
                        last = pe.matmul(ps[:, bank, :], lhsT=xb[:, kc, tt * 128:(tt + 1) * 128], rhs=slab[:, kc, :],
                                         start=(kc == 0), stop=(kc == 15))
                    return last
                S.op("pe", fn, reads=[sbuf_] + B_xb, writes=[B_ps[bank]])
                j = k[0] % 2
                k[0] += 1
                if tt % 2 == 0:
                    S.op("act", lambda e: e.activation(out=stg[j][:, 0:512], in_=ps[:, bank, :], func=AF.Copy),
                         reads=[B_ps[bank]], writes=[B_stg[j]])
                else:
                    S.op("dve", lambda e: e.tensor_copy(out=stg[j][:, 0:512], in_=ps[:, bank, :]),
                         reads=[B_ps[bank]], writes=[B_stg[j]])
                hp, c0 = s // 2, (s % 2) * 512
                S.dma("sp", f"stg{j}", lambda e: e.dma_start(out=v_own[hp][tt * 128:(tt + 1) * 128, c0:c0 + 512], in_=stg[j][:, 0:512]),
                      reads=[B_stg[j]], writes=[B_vo[hp][j]])

            for s in range(4):
                slab, sbuf_ = ws.get(wslabA(wq, 2 * D + s * 512), "A")
                if S.dry:
                    continue
                for tt in range(8):
                    v_tile(s, tt, slab, sbuf_)
                if s % 2 == 1:
                    gather("v", s // 2, v_own[s // 2], v_g[s // 2], B_vo[s // 2], B_vg[s // 2])
            if fox and not S.dry:
                fox_ctx_bias()

        def fox_forget(i):
            j = i // 3
            cn, tmpe, ones48 = AF32(12), AF32(13), AF32(9)
            wfv = wf_t[:, :].rearrange("p (k m) -> p k m", k=16)
            S.dma("pool", "wf", lambda e: e.dma_start(out=wf_t[:, :], in_=W[i, "wf"]), reads=(), writes=[B_wf])
            pair = next_pair()

            def fz(pe):
                last = None
                for kc in range(16):
                    for tb in range(2):
                        last = pe.matmul(ps[0:48, pair[tb], :], lhsT=wfv[:, kc, :], rhs=xb[:, kc, tb * 512:(tb + 1) * 512],
                                         start=(kc == 0), stop=(kc == 15))
                return last
            S.op("pe", fz, reads=[B_wf] + B_xb, writes=[B_ps[pair[0]], B_ps[pair[1]]])
            S.op("act", lambda e: e.activation(out=v2(tmpe[0:48, :]), in_=ps[0:48, pair[0]:pair[0] + 2, :], func=AF.Exp,
                                               bias=nbf[0:48, j:j + 1], scale=-1.0),
                 reads=[B_ps[pair[0]], B_ps[pair[1]], B_nbf], writes=AB(13))
            S.op("act", lambda e: e.activation(out=tmpe[0:48, :], in_=tmpe[0:48, :], func=AF.Ln, bias=onec[0:48, 0:1], scale=1.0),
                 reads=AB(13) + [B_cst], writes=AB(13))
            S.op("pool", lambda e: e.memset(ones48[0:48, :], 1.0), reads=(), writes=AB(9))
            S.op("dve", lambda e: e.tensor_tensor_scan(out=cn[0:48, :], data0=ones48[0:48, :], data1=tmpe[0:48, :], initial=0.0,
                                                        op0=ALU.mult, op1=ALU.add),
                 reads=AB(9) + AB(13), writes=AB(12))
            S.dma("sp", "cown", lambda e: e.dma_start(out=c_own[:, :], in_=cn[0:16, :]), reads=AB(12), writes=[B_co])
            S.dma("pool", "cc_c", lambda e: e.collective_compute("AllGather", ALU.bypass, replica_groups=GROUPS,
                                                               ins=[c_own[:, :]], outs=[c_g[:, :]]),
                  reads=[B_co], writes=[B_cg], inc=1)
            hl = ABF(11)[:, 0:1024]
            hi32 = ABF(11)[:, 1024:2048]
            S.op("act", lambda e: e.activation(out=hl[0:48, :], in_=cn[0:48, :], func=AF.Copy, scale=-1.0),
                 reads=AB(12), writes=AB(11))
            S.op("act", lambda e: e.activation(out=hi32[32:48, :], in_=cn[32:48, :], func=AF.Copy, scale=-1.0),
                 reads=AB(12), writes=AB(11))
            S.op("dve", lambda e: e.scalar_tensor_tensor(out=hl[32:48, :], in0=cn[32:48, :], scalar=-1.0, in1=hi32[32:48, :],
                                                         op0=ALU.mult, op1=ALU.subtract),
                 reads=AB(12) + AB(11), writes=AB(11))
            pr = next_pair()

            def ft(pe):
                last = None
                for t_ in range(8):
                    last = pe.matmul(ps[:, pr[0], t_ * 16:(t_ + 1) * 16], lhsT=cn[0:16, t_ * 128:(t_ + 1) * 128], rhs=ident[:, :],
                                     start=True, stop=True)
                return last
            S.op("pe", ft, reads=AB(12) + [B_cst], writes=[B_ps[pr[0]]])
            S.op("dve", lambda e: e.tensor_copy(out=kb[:, 0:128], in_=ps[:, pr[0], 0:128]), reads=[B_ps[pr[0]]], writes=[B_kb])

        def fox_ctx_bias():
            cctx = AF32(13)
            S.dma("sp", "cctx", lambda e: e.dma_start(out=cctx[0:16, :], in_=c_g[0:16, :]), reads=[B_cg], writes=AB(13))
            S.op("dve", lambda e: e.tensor_scalar(out=cctx[0:16, :], in0=cctx[0:16, :], scalar1=cctx[0:16, 1023:1024], scalar2=None,
                                                  op0=ALU.subtract),
                 reads=AB(13), writes=AB(13))
            pr2 = next_pair()

            def ft2(pe):
                last = None
                for t_ in range(8):
                    last = pe.matmul(ps[:, pr2[0], t_ * 16:(t_ + 1) * 16], lhsT=cctx[0:16, t_ * 128:(t_ + 1) * 128], rhs=ident[:, :],
                                     start=True, stop=True)
                return last
            S.op("pe", ft2, reads=AB(13) + [B_cst], writes=[B_ps[pr2[0]]])
            S.op("dve", lambda e: e.tensor_scalar(out=kb[:, 128:256], in0=ps[:, pr2[0], 0:128], scalar1=ppc(PP_CTXMASK), scalar2=None,
                                                  op0=ALU.add),
                 reads=[B_ps[pr2[0]], B_pp], writes=[B_kb])

        def load_kv(h, slot, ctx_lo):
            kt = ABF(2 + slot)
            vt = ABF(4 + slot).rearrange("p (t d) -> p t d", t=16)
            hp, rr = h // 8, (h % 8) * 128
            t0 = ctx_lo // 128
            S.dma("sp", f"kt{slot}", lambda e: e.dma_start(out=kt[:, ctx_lo:1024], in_=kT_g[hp][rr:rr + 128, ctx_lo:1024]),
                  reads=[B_kTg[hp]], writes=AB(2 + slot))
            S.dma("sp", f"kt{slot}", lambda e: e.dma_start(out=kt[:, 1024:2048], in_=kT_own[hp][rr:rr + 128, :]),
                  reads=B_kTo[hp], writes=AB(2 + slot))
            S.dma("sp", f"vt{slot}", lambda e: e.dma_start(
                out=vt[:, t0:8, :], in_=v_g[hp][ctx_lo:1024, rr:rr + 128].rearrange("(t p) d -> p t d", p=128)),
                reads=[B_vg[hp]], writes=AB(4 + slot))
            S.dma("sp", f"vt{slot}", lambda e: e.dma_start(
                out=vt[:, 8:16, :], in_=v_own[hp][:, rr:rr + 128].rearrange("(t p) d -> p t d", p=128)),
                reads=B_vo[hp], writes=AB(4 + slot))
            return kt, vt

        def q_proj(i, hgp, qT):
            slab, sbuf_ = ws.get(wslabA(W[i, "qkv"], hgp * 512), "A")

            def ev_q(nch, pair):
                S.op("act", lambda e: e.activation(out=v2(qT[:, nch, :]), in_=ps2(pair), func=AF.Copy, scale=QSCALE),
                     reads=[B_ps[pair[0]], B_ps[pair[1]]], writes=AB(0, 2))
            if not S.dry:
                proj_fm(slab, sbuf_, ev_q)

        def fox_layer(i):
            attn_phase_a(i, True)
            qT = ABF(0, 2).rearrange("p (k t) -> p k t", k=4)
            ob = ABF(8, 2).rearrange("p (k t) -> p k t", k=4)
            pT = [ABF(6)[:, s * 512:(s + 1) * 512] for s in range(4)]
            selv = ABF(10).rearrange("p (h m) -> p h m", h=16)
            hl = ABF(11)[:, 0:1024]
            state["pairs"] = [(0, 1), (2, 3)]
            kvs = {}
            if not S.dry:
                S.dma("pool", "sel", lambda e: e.dma_start(out=ABF(10)[0:48, :], in_=cstd[0:48, CS_SEL:CS_SEL + 2048]),
                      reads=(), writes=AB(10))
                kvs[0] = load_kv(0, 0, 0)

            qbc = [0]

            def qblock(h, hh, kt, vt, slot, qb):
                bo, bl = (6, 7) if qbc[0] % 2 == 0 else (2, 3)
                qbc[0] += 1
                tiles = [(j, 0, False) for j in range(8)]
                for jo in range(4 * qb + 4):
                    off = max(0, (jo - 4 * qb) * 128)
                    tiles.append((8 + jo, off, jo >= 4 * qb))
                nt = len(tiles)
                q0 = qb * 512

                def qk(idx):
                    j, off, diag = tiles[idx]
                    bank = 4 + idx % 2

                    def fn(pe):
                        pe.matmul(ps[:, bank, off:512], lhsT=kt[:, j * 128:(j + 1) * 128], rhs=qT[:, hh, q0 + off:q0 + 512],
                                  start=True, stop=False, skip_group_check=True)
                        return pe.matmul(ps[:, bank, off:512], lhsT=selv[0:48, h, :], rhs=hl[0:48, q0 + off:q0 + 512],
                                         start=False, stop=True, skip_group_check=True)
                    S.op("pe", fn, reads=AB(2 + slot) + AB(0, 2) + AB(10) + AB(11), writes=[B_ps[bank]])
                    col = (128 + j * 16 + h) if j < 8 else ((j - 8) * 16 + h)
                    pt = pT[idx % 4]
                    bpt = [B_pT[idx % 4]]
                    S.op("act", lambda e: e.activation(out=pt[:, off:512], in_=ps[:, bank, off:512], func=AF.Exp,
                                                       bias=kb[:, col:col + 1], scale=1.0),
                         reads=[B_ps[bank], B_kb], writes=bpt)
                    if diag:
                        S.op("dve", lambda e: e.tensor_tensor(out=pt[:, off:off + 128], in0=pt[:, off:off + 128], in1=tri_b[:, :],
                                                              op=ALU.mult),
                             reads=bpt + [B_cst], writes=bpt)

                def pv(idx):
                    j, off, diag = tiles[idx]
                    pt = pT[idx % 4]

                    def fn(pe):
                        pe.matmul(ps[:, bo, off:512], lhsT=vt[:, j, :], rhs=pt[:, off:512], start=(idx == 0), stop=(idx == nt - 1),
                                  skip_group_check=True)
                        return pe.matmul(ps[:, bl, off:512], lhsT=ones_b[:, :], rhs=pt[:, off:512], start=(idx == 0),
                                         stop=(idx == nt - 1), skip_group_check=True)
                    S.op("pe", fn, reads=AB(4 + slot) + [B_pT[idx % 4], B_cst], writes=[B_ps[bo], B_ps[bl]])
                qk(0)
                for idx in range(nt):
                    if idx + 1 < nt:
                        qk(idx + 1)
                    pv(idx)
                S.op("dve", lambda e: e.reciprocal(out=rl[:, :], in_=ps[:, bl, :]), reads=[B_ps[bl]], writes=[B_rl])
                S.op("dve", lambda e: e.tensor_tensor(out=ob[:, hh, q0:q0 + 512], in0=ps[:, bo, :], in1=rl[:, :], op=ALU.mult),
                     reads=[B_ps[bo], B_rl], writes=AB(8, 2))

            for hgp in range(4):
                q_proj(i, hgp, qT)
                for hh in range(4):
                    if S.dry:
                        continue
                    h = hgp * 4 + hh
                    kt, vt = kvs[h]
                    if h + 1 < 16:
                        kvs[h + 1] = load_kv(h + 1, (h + 1) % 2, 0)
                    for qb in range(2):
                        qblock(h, hh, kt, vt, h % 2, qb)
                slab, sbuf_ = ws.get(wslabB(W[i, "o"], hgp * 512), "B")
                if not S.dry:
                    proj_acc(ob, AB(8, 2), 4, slab, sbuf_, last=(hgp == 3))
            state["pairs"] = pairs_all

        def rel_layer(i):
            attn_phase_a(i, False)
            qT = ABF(0, 2).rearrange("p (k t) -> p k t", k=4)
            ob = ABF(8, 2).rearrange("p (k t) -> p k t", k=4)
            pT5 = [ABF(6)[:, 0:640], ABF(6)[:, 1024:1664]]
            sbt = [AF32(7)[:, 0:640], AF32(11)[:, 0:640]]
            sbb = [AB(7), AB(11)]
            relt = [AF32(12)[:, 0:640], AF32(13)[:, 0:640]]
            state["pairs"] = [(0, 1), (2, 3)]
            sc = [(4, 5), (6, 7)]
            kvs = {}
            if not S.dry:
                kvs[0] = load_kv(0, 0, 512)
                S.dma("sp", "relb0", lambda e: e.dma_start(out=relt[0], in_=W[i, "relb"][0, :, :]), reads=(), writes=AB(12))
            cnt = [0]

            def qtile(h, hh, kt, vt, slot, qi):
                c = cnt[0] % 2
                cnt[0] += 1
                bA, bB = sc[c]
                bO = 2 * c
                cB = 0
                rt = relt[slot]
                rtb = AB(12 + slot)
                nctx = max(0, 4 - qi)
                ktiles = [8 + qi - 4 + r for r in range(5)]
                sbc, sbcb = sbt[c], sbb[c]
                pt = pT5[c]
                bpt = [B_pT[c]]
                oc = 0

                def qk_part():
                    def fqk(pe):
                        last = None
                        for r in range(5):
                            dst = ps[:, bA, r * 128:(r + 1) * 128] if r < 4 else ps[:, bB, cB:cB + 128]
                            j = ktiles[r]
                            last = pe.matmul(dst, lhsT=kt[:, j * 128:(j + 1) * 128], rhs=qT[:, hh, qi * 128:(qi + 1) * 128],
                                             start=True, stop=True, skip_group_check=True)
                        return last
                    S.op("pe", fqk, reads=AB(2 + slot) + AB(0, 2), writes=[B_ps[bA], B_ps[bB]])
                    flat = ps[:, bA:bA + 2, :].rearrange("p a b -> p (a b)")
                    S.op("dve", lambda e: e.tensor_tensor(out=sbc[:, 0:640], in0=flat[:, 0:640], in1=rt[:, 0:640], op=ALU.add),
                         reads=[B_ps[bA], B_ps[bB]] + rtb, writes=sbcb)
                    nc_ = nctx * 128
                    if nctx > 0:
                        S.op("act", lambda e: e.activation(out=pt[:, 0:nc_], in_=sbc[:, 0:nc_], func=AF.Exp, bias=ppc(PP_CTXMASK), scale=1.0),
                             reads=sbcb + [B_pp], writes=bpt)
                    S.op("act", lambda e: e.activation(out=pt[:, nc_:640], in_=sbc[:, nc_:640], func=AF.Exp), reads=sbcb, writes=bpt)

                def pv_part():
                    def fpv(pe):
                        last = None
                        for r in range(5):
                            j = ktiles[r]
                            pe.matmul(ps[:, bO, oc:oc + 128], lhsT=vt[:, j, :], rhs=pt[:, r * 128:(r + 1) * 128],
                                      start=(r == 0), stop=(r == 4), skip_group_check=True)
                        for r in range(5):
                            last = pe.matmul(ps[:, bO, oc + 128:oc + 256], lhsT=ones_b[:, :], rhs=pt[:, r * 128:(r + 1) * 128],
                                             start=(r == 0), stop=(r == 4), skip_group_check=True)
                        return last
                    S.op("pe", fpv, reads=AB(4 + slot) + bpt + [B_cst], writes=[B_ps[bO]])
                    rc = c * 128
                    S.op("dve", lambda e: e.reciprocal(out=rl[:, rc:rc + 128], in_=ps[:, bO, oc + 128:oc + 256]),
                         reads=[B_ps[bO]], writes=[B_rl2[c]])
                    S.op("dve", lambda e: e.tensor_tensor(out=ob[:, hh, qi * 128:(qi + 1) * 128], in0=ps[:, bO, oc:oc + 128],
                                                          in1=rl[:, rc:rc + 128], op=ALU.mult),
                         reads=[B_ps[bO], B_rl2[c]], writes=AB(8, 2))
                return qk_part, pv_part

            def relb_load(hn):
                S.dma("sp", f"relb{hn % 2}", lambda e: e.dma_start(out=relt[hn % 2], in_=W[i, "relb"][hn, :, :]),
                      reads=(), writes=AB(12 + hn % 2))

            if not S.dry:
                kvs[1] = load_kv(1, 1, 512)
                relb_load(1)
            for hgp in range(4):
                q_proj(i, hgp, qT)
                if not S.dry:
                    steps = []
                    for hh in range(4):
                        h = hgp * 4 + hh
                        kt = ABF(2 + h % 2)
                        vt = ABF(4 + h % 2).rearrange("p (t d) -> p t d", t=16)
                        for qi in range(8):
                            steps.append((h, qi) + qtile(h, hh, kt, vt, h % 2, qi))
                    skew = not DEBUG.get("noskew")
                    if skew:
                        steps[0][2]()
                    for t_, (h, qi, qk_, pv_) in enumerate(steps):
                        if not skew:
                            qk_()
                        elif t_ + 1 < len(steps):
                            steps[t_ + 1][2]()
                        pv_()
                        if qi == 7 and h + 2 < 16:
                            kvs[h + 2] = load_kv(h + 2, h % 2, 512)
                            relb_load(h + 2)
                slab, sbuf_ = ws.get(wslabB(W[i, "o"], hgp * 512), "B")
                if not S.dry:
                    proj_acc(ob, AB(8, 2), 4, slab, sbuf_, last=(hgp == 3))
            state["pairs"] = pairs_all

        def conv_layer(i):
            cpt = AF32(13)[:, 0:NCP]
            ub = ABF(0, 3)[:, 0:4 * 1056].rearrange("p (k t) -> p k t", k=4)
            sg = [AF32(3), AF32(4)]
            dg = [ABF(5, 2)[:, 0:31 * 128].rearrange("p (k m) -> p k m", k=31), ABF(7, 2)[:, 0:31 * 128].rearrange("p (k m) -> p k m", k=31)]
            dgb = [AB(5, 2), AB(7, 2)]
            acc, sq, s1c, s2c = AF32(9), AF32(10), AF32(11), AF32(12)
            hal = rl[:, 0:64].bitcast(BF16).rearrange("p (k t) -> p k t", k=4)
            if not S.dry:
                S.dma("sp", "cp", lambda e: e.dma_start(out=cpt, in_=W[i, "cp"]), reads=(), writes=AB(13))

            gk = [0]

            def glu_unit(g, nch, tb, slab_g, sb_g, slab_a, sb_a):
                state["after_ln"] = False
                fcg = g * 4 + nch
                pair = next_pair()
                j = gk[0] % 2
                gk[0] += 1

                def fn(pe):
                    last = None
                    for which, slab in ((0, slab_g), (1, slab_a)):
                        for kc in range(16):
                            last = pe.matmul(ps[:, pair[which], :], lhsT=slab[:, kc, nch * 128:(nch + 1) * 128],
                                             rhs=xb[:, kc, tb * 512:(tb + 1) * 512], start=(kc == 0), stop=(kc == 15))
                    return last
                S.op("pe", fn, reads=[sb_g, sb_a] + B_xb, writes=[B_ps[pair[0]], B_ps[pair[1]]])
                S.op("act", lambda e: e.activation(out=sg[j][:, 0:512], in_=ps[:, pair[0], :], func=AF.Sigmoid,
                                                   bias=cpt[:, CP_BPW1 + 16 + fcg:CP_BPW1 + 17 + fcg], scale=1.0),
                     reads=[B_ps[pair[0]]] + AB(13), writes=AB(3 + j))
                c0 = 32 + tb * 512
                S.op("dve", lambda e: e.scalar_tensor_tensor(
                    out=ub[:, nch, c0:c0 + 512], in0=ps[:, pair[1], :], scalar=cpt[:, CP_BPW1 + fcg:CP_BPW1 + fcg + 1],
                    in1=sg[j][:, 0:512], op0=ALU.add, op1=ALU.mult),
                    reads=[B_ps[pair[1]]] + AB(13) + AB(3 + j), writes=AB(0, 3))

            def halo_send(g):
                S.dma("sp", "halo", lambda e: e.dma_start(out=halo_own[g].rearrange("(k p) t -> p k t", p=128), in_=ub[:, :, 1024:1056]),
                      reads=AB(0, 3), writes=[B_ho[g]])
                S.dma("pool", f"cc_h{g}", lambda e: e.collective_compute("AllGather", ALU.bypass, replica_groups=GROUPS,
                                                                         ins=[halo_own[g][:, :]], outs=[halo_g[g][:, :]]),
                      reads=[B_ho[g]], writes=[B_hg[g]], inc=1)

            def halo_recv(g):
                S.dma("sp", "halo", lambda e: e.dma_start(out=hal, in_=halo_g[g][0:512, :].rearrange("(k p) t -> p k t", p=128)),
                      reads=[B_hg[g]], writes=[B_rl])
                S.op("dve", lambda e: e.tensor_scalar(out=ub[:, :, 0:32], in0=hal, scalar1=ppc(PP_HASPREV), scalar2=None, op0=ALU.mult),
                     reads=[B_rl, B_pp], writes=AB(0, 3))

            conv_pairs = [(0, 1), (2, 3), (4, 5), (6, 7)]

            def diag_build(g, nch):
                fcg = g * 4 + nch
                wc = CP_WDW + fcg * 31
                d_, db_ = dg[nch % 2], dgb[nch % 2]

                def fdiag(e):
                    last = None
                    for k_ in range(31):
                        last = e.tensor_scalar(out=d_[:, k_, :], in0=ident_b[:, :], scalar1=cpt[:, wc + k_:wc + k_ + 1], scalar2=None,
                                               op0=ALU.mult)
                    return last
                S.op("dve", fdiag, reads=AB(13) + [B_cst], writes=db_)

            def conv_mm(g, nch, tb):
                d_, db_ = dg[nch % 2], dgb[nch % 2]
                bank = conv_pairs[nch][tb]

                def fconv(pe):
                    last = None
                    for k_ in range(31):
                        c0 = 2 + k_ + tb * 512
                        last = pe.matmul(ps[:, bank, :], lhsT=d_[:, k_, :], rhs=ub[:, nch, c0:c0 + 512],
                                         start=(k_ == 0), stop=(k_ == 30))
                    return last
                S.op("pe", fconv, reads=db_ + AB(0, 3), writes=[B_ps[bank]])

            def conv_evac(g, nch):
                fcg = g * 4 + nch
                pair = conv_pairs[nch]
                S.op("act", lambda e: e.activation(out=v2(acc), in_=ps2(pair), func=AF.Identity,
                                                   bias=cpt[:, CP_BDW + fcg:CP_BDW + fcg + 1], scale=1.0),
                     reads=[B_ps[pair[0]], B_ps[pair[1]]] + AB(13), writes=AB(9))
                first = (g == 0 and nch == 0)
                if first:
                    S.op("dve", lambda e: e.tensor_copy(out=s1c, in_=acc), reads=AB(9), writes=AB(11))
                else:
                    S.op("dve", lambda e: e.tensor_tensor(out=s1c, in0=s1c, in1=acc, op=ALU.add), reads=AB(9) + AB(11), writes=AB(11))
                S.op("act", lambda e: e.activation(out=sq, in_=acc, func=AF.Square), reads=AB(9), writes=AB(10))
                if first:
                    S.op("dve", lambda e: e.tensor_copy(out=s2c, in_=sq), reads=AB(10), writes=AB(12))
                else:
                    S.op("dve", lambda e: e.tensor_tensor(out=s2c, in0=s2c, in1=sq, op=ALU.add), reads=AB(10) + AB(12), writes=AB(12))
                S.dma("sp", "ysc", lambda e: e.dma_start(out=yscr[fcg * 128:(fcg + 1) * 128, :], in_=acc),
                      reads=AB(9), writes=[B_y[0]])

            for g in range(4):
                slab_g, sb_g = ws.get(wslabA(W[i, "pw1"], D + g * 512), "A")
                slab_a, sb_a = ws.get(wslabA(W[i, "pw1"], g * 512), "A", live=1)
                if S.dry:
                    continue
                for tb in (1, 0):
                    for nch in range(4):
                        glu_unit(g, nch, tb, slab_g, sb_g, slab_a, sb_a)
                    if tb == 1:
                        halo_send(g)
                for cp_ in range(2):
                    n0, n1 = 2 * cp_, 2 * cp_ + 1
                    diag_build(g, n0)
                    diag_build(g, n1)
                    conv_mm(g, n0, 1)
                    conv_mm(g, n1, 1)
                    if cp_ == 0:
                        halo_recv(g)
                    conv_mm(g, n0, 0)
                    conv_evac(g, n0)
                    conv_mm(g, n1, 0)
                    conv_evac(g, n1)
            mean, rstd = AF32(8), AF32(9)
            if DEBUG.get("conv_y"):
                if not S.dry:
                    tok = S.dma("sp", "dbg", lambda e: e.dma_start(out=outT[:, :], in_=yscr[:, :]), reads=B_y, writes=[B_out])
                    S.final.append((tok[0], tok[1]))
                return
            if not S.dry:
                ln_stats_finish(s1c, s2c, B_ar[11], B_ar[12], mean, rstd, B_ar[8], B_ar[9], AF32(10), B_ar[10])
            yb = AF32(0, 4).rearrange("p (k t) -> p k t", k=4)
            tt_ = [AF32(4), AF32(5)]
            hbvs = [(ABF(6, 2).rearrange("p (k t) -> p k t", k=4), AB(6, 2)), (ABF(10, 2).rearrange("p (k t) -> p k t", k=4), AB(10, 2))]

            def norm_chunk(g, nch):
                hbv, hbb = hbvs[1 if g == 1 else 0]
                fcg = g * 4 + nch
                t1 = tt_[nch % 2]
                tb_ = AB(4 + nch % 2)
                S.op("dve", lambda e: e.tensor_tensor(out=t1, in0=yb[:, nch, :], in1=mean, op=ALU.subtract),
                     reads=AB(0, 4) + AB(8), writes=tb_)
                S.op("dve", lambda e: e.tensor_tensor(out=t1, in0=t1, in1=rstd, op=ALU.mult), reads=tb_ + AB(9), writes=tb_)
                S.op("act", lambda e: e.activation(out=hbv[:, nch, :], in_=t1, func=AF.Silu,
                                                   scale=cpt[:, CP_LNG + fcg:CP_LNG + fcg + 1],
                                                   bias=cpt[:, CP_LNB + fcg:CP_LNB + fcg + 1]),
                     reads=tb_ + AB(13), writes=hbb)

            def yload(g):
                S.dma("sp", "yld", lambda e: e.dma_start(out=yb, in_=yscr[g * 512:(g + 1) * 512, :].rearrange("(k p) t -> p k t", p=128)),
                      reads=B_y, writes=AB(0, 4))

            for g in range(4):
                slab, sbuf_ = ws.get(wslabB(W[i, "pw2"], g * 512), "B")
                if S.dry:
                    continue
                yload(g)
                for nch in range(4):
                    norm_chunk(g, nch)
                hbv, hbb = hbvs[1 if g == 1 else 0]
                if g == 0:
                    proj_acc(hbv, hbb, 4, slab, sbuf_)
                    for n in range(16):
                        S.op("dve", lambda e, n=n: e.tensor_scalar(out=xr[:, n, :], in0=xr[:, n, :],
                                                                   scalar1=cpt[:, CP_BPW2 + n:CP_BPW2 + n + 1], scalar2=None, op0=ALU.add),
                             reads=[B_xr[n]] + AB(13), writes=[B_xr[n]])
                else:
                    proj_acc(hbv, hbb, 4, slab, sbuf_, last=(g == 3))

        epsc = sb("epsc", [128, 1], F32)
        onec = sb("onec", [128, 1], F32)

        def prologue():
            S.dma("sp", "pp", lambda e: e.dma_start(out=pp[:, :], in_=ppd[:, :]), reads=(), writes=[B_pp])
            S.dma("sp", "cst", lambda e: e.dma_start(out=ones_f[:, :], in_=cstd[:, CS_ONES:CS_ONES + 128]), reads=(), writes=[B_cst])
            S.dma("sp", "cst", lambda e: e.dma_start(out=ident[:, :], in_=cstd[0:16, CS_ID:CS_ID + 16]), reads=(), writes=[B_cst])
            S.dma("pool", "cstb", lambda e: e.dma_start(out=ones_b[:, :], in_=cstd[:, CS_ONES:CS_ONES + 128]), reads=(), writes=[B_cst])
            S.dma("pool", "cstb", lambda e: e.dma_start(out=tri_b[:, :], in_=cstd[:, CS_TRI:CS_TRI + 128]), reads=(), writes=[B_cst])
            S.dma("pool", "cstb", lambda e: e.dma_start(out=ident_b[:, :], in_=cstd[:, CS_IDB:CS_IDB + 128]), reads=(), writes=[B_cst])
            S.op("pool", lambda e: e.memset(epsc[:, :], LN_EPS), reads=(), writes=[B_cst])
            S.op("pool", lambda e: e.memset(onec[:, :], 1.0), reads=(), writes=[B_cst])
            S.op("dve", lambda e: e.tensor_scalar(out=nbf[:, :], in0=pp[0:48, PP_BF:PP_BF + 2], scalar1=-1.0, scalar2=None, op0=ALU.mult),
                 reads=[B_pp], writes=[B_nbf])
            for fc in range(FC):
                S.dma("sp", f"xin{fc}", lambda e, fc=fc: e.dma_start(out=xr[:, fc, :], in_=xT[fc * 128:(fc + 1) * 128, :]),
                      reads=(), writes=[B_xr[fc]])
                S.op("act" if fc % 2 == 0 else "dve",
                     (lambda e, fc=fc: e.activation(out=xb[:, fc, :], in_=xr[:, fc, :], func=AF.Copy)) if fc % 2 == 0 else
                     (lambda e, fc=fc: e.tensor_copy(out=xb[:, fc, :], in_=xr[:, fc, :])),
                     reads=[B_xr[fc]], writes=[B_xb[fc]])

        def model():
            if not S.dry:
                prologue()
                state["after_ln"] = True
            last = layers[-1]
            for i in layers:
                kind = i % 3
                if do_mixer:
                    if kind == 0:
                        fox_layer(i)
                    elif kind == 1:
                        rel_layer(i)
                    else:
                        conv_layer(i)
                        if DEBUG.get("conv_y"):
                            break
                    if not S.dry:
                        layer_norm(PP_LN + i * 64, PP_LN + i * 64 + 16, final=(not do_ffn and i == last))
                if do_ffn:
                    ffn(i)
                    if not S.dry:
                        layer_norm(PP_LN + i * 64 + 32, PP_LN + i * 64 + 48, final=(i == last))

        S.dry = True
        model()
        S.dry = False
        state["pair"] = 0
        state["first_acc"] = True
        model()
        S.emit()
    return nc


def _cols(v):
    v = np.asarray(v, np.float32)
    return np.ascontiguousarray(v.reshape(-1, 128).T)


def _consts():
    c = np.zeros((128, NCS), np.float32)
    c[:, CS_ONES:CS_ONES + 128] = 1.0
    k = np.arange(128)[:, None]
    q = np.arange(128)[None, :]
    c[:, CS_TRI:CS_TRI + 128] = (q >= k).astype(np.float32)
    c[0:16, CS_ID:CS_ID + 16] = np.eye(16, dtype=np.float32)
    c[:, CS_IDB:CS_IDB + 128] = np.eye(128, dtype=np.float32)
    for h in range(16):
        c[h, CS_SEL + h * 128:CS_SEL + (h + 1) * 128] = 1.0
        c[32 + h, CS_SEL + h * 128:CS_SEL + (h + 1) * 128] = 1.0
    return c


def _relb_table(rel_bias):
    rb = np.asarray(rel_bias, np.float32)
    r = np.arange(5)[:, None, None]
    p = np.arange(128)[None, :, None]
    j = np.arange(128)[None, None, :]
    kpos = (r - 4) * 128 + p
    kch = np.floor_divide(kpos, 64)
    qch = j // 64
    valid = (kch >= qch - 8) & (kch <= qch)
    idx = np.clip(j - kpos, -128, 128) + 128
    idx, valid = np.broadcast_arrays(idx, valid)
    tab = rb[:, idx]
    tab = np.where(valid[None], tab, np.float32(NEG)).astype(np.float32)
    return np.ascontiguousarray(tab.transpose(0, 2, 1, 3).reshape(H, 128, 640))


def make_in_maps(inp, layers, do_mixer=True, do_ffn=True):
    x = np.asarray(inp["x"], np.float32)
    shared = {"cst": _consts()}
    pp = np.zeros((128, NPP), np.float32)
    for i in range(DEPTH):
        b = PP_LN + i * 64
        pp[:, b:b + 16] = _cols(inp["ln_mix_g"][i])
        pp[:, b + 16:b + 32] = _cols(inp["ln_mix_b"][i])
        pp[:, b + 32:b + 48] = _cols(inp["ln_ffn_g"][i])
        pp[:, b + 48:b + 64] = _cols(inp["ln_ffn_b"][i])
    for j in range(2):
        bf = np.asarray(inp["fox_b_f"][j], np.float32)
        pp[0:16, PP_BF + j] = bf
        pp[32:48, PP_BF + j] = bf
    for i in layers:
        kind, j = i % 3, i // 3
        if do_mixer:
            if kind == 0:
                shared[f"wqkv{i}"] = np.asarray(inp["fox_w_qkv"][j], np.float32)
                shared[f"wo{i}"] = np.asarray(inp["fox_w_o"][j], np.float32)
                wf = np.asarray(inp["fox_w_f"][j], np.float32)
                wfe = np.zeros((D, 48), np.float32)
                wfe[:, 0:16] = wf
                wfe[:, 32:48] = wf
                shared[f"wf{i}"] = np.ascontiguousarray(wfe.reshape(16, 128, 48).transpose(1, 0, 2).reshape(128, 16 * 48))
            elif kind == 1:
                shared[f"wqkv{i}"] = np.asarray(inp["rel_w_qkv"][j], np.float32)
                shared[f"wo{i}"] = np.asarray(inp["rel_w_o"][j], np.float32)
                shared[f"relb{i}"] = _relb_table(inp["rel_bias"][j])
            else:
                shared[f"pw1{i}"] = np.asarray(inp["conv_w_pw1"][j], np.float32)
                shared[f"pw2{i}"] = np.asarray(inp["conv_w_pw2"][j], np.float32)
                cp = np.zeros((128, NCP), np.float32)
                cp[:, CP_BPW1:CP_BPW1 + 32] = _cols(inp["conv_b_pw1"][j])
                wdw = np.asarray(inp["conv_w_dw"][j], np.float32)
                cp[:, CP_WDW:CP_WDW + 496] = wdw.reshape(31, 16, 128).transpose(2, 1, 0).reshape(128, 496)
                cp[:, CP_BDW:CP_BDW + 16] = _cols(inp["conv_b_dw"][j])
                cp[:, CP_LNG:CP_LNG + 16] = _cols(inp["conv_ln_g"][j])
                cp[:, CP_LNB:CP_LNB + 16] = _cols(inp["conv_ln_b"][j])
                cp[:, CP_BPW2:CP_BPW2 + 16] = _cols(inp["conv_b_pw2"][j])
                shared[f"cp{i}"] = cp
        if do_ffn:
            shared[f"wg{i}"] = np.asarray(inp["ffn_w_gate"][i], np.float32)
            shared[f"wu{i}"] = np.asarray(inp["ffn_w_up"][i], np.float32)
            shared[f"wd{i}"] = np.asarray(inp["ffn_w_down"][i], np.float32)
    maps = []
    for c in range(NCORES):
        b, hf = c // 2, c % 2
        m = dict(shared)
        m["xT"] = np.ascontiguousarray(x[b, hf * T:(hf + 1) * T, :].T)
        ppc_ = pp.copy()
        ppc_[:, PP_CTXMASK] = 0.0 if hf == 1 else NEG
        ppc_[:, PP_HASPREV] = 1.0 if hf == 1 else 0.0
        m["pp"] = ppc_
        maps.append(m)
    return maps


def run(inp, layers, do_mixer=True, do_ffn=True):
    nc = build(layers, do_mixer, do_ffn)
    maps = make_in_maps(inp, layers, do_mixer, do_ffn)
    res = run_bass_kernel_spmd(nc, maps, core_ids=list(range(NCORES)))
    out = np.empty((4, 2 * T, D), np.float32)
    for c in range(NCORES):
        b, hf = c // 2, c % 2
        out[b, hf * T:(hf + 1) * T, :] = np.asarray(res.results[c]["outT"]).T
    return out


def kernel(**inputs):
    return run(inputs, [0, 1, 2, 3])
```

```python
import contextlib
import math
import numpy as np
import concourse.bass as bass
import concourse.mybir as mybir
from concourse.bass_utils import run_bass_kernel_spmd

F32 = mybir.dt.float32
BF16 = mybir.dt.bfloat16
AF = mybir.ActivationFunctionType
ALU = mybir.AluOpType
AX = mybir.AxisListType

D = 2048
T = 1024
FC = 16
H = 16
F = 5632
NG = F // 512
DEPTH = 4
ALPHA = (2.0 * DEPTH) ** 0.25
LN_EPS = 1e-5
QSCALE = 128 ** -0.5
NEG = -30000.0
NBLK = 14
NCORES = 8
GROUPS = [[0, 1], [2, 3], [4, 5], [6, 7]]
DEBUG = {}

PP_LN = 0
PP_BF = 256
PP_CTXMASK = 258
PP_HASPREV = 259
NPP = 260
CP_BPW1 = 0
CP_WDW = 32
CP_BDW = 528
CP_LNG = 544
CP_LNB = 560
CP_BPW2 = 576
NCP = 592
CS_ONES = 0
CS_TRI = 128
CS_ID = 256
CS_SEL = 272
CS_IDB = 272 + 2048
NCS = 272 + 2048 + 128


class Buf:
    __slots__ = ("name", "w", "rs")

    def __init__(self, name):
        self.name = name
        self.w = None
        self.rs = {}


class Op:
    __slots__ = ("waits", "fn", "inc")

    def __init__(self, waits, fn, inc):
        self.waits, self.fn, self.inc = waits, fn, inc


ENG = ["pe", "act", "dve", "pool", "sp"]


class Sched:
    def __init__(self, nc, stack):
        self.nc, self.stack = nc, stack
        self.ops = {e: [] for e in ENG}
        self.sems = {}
        self.count = {}
        self.eng_key = {}
        self.eng_n = {e: 0 for e in ENG}
        self.seen = {e: {} for e in ENG}
        self.dry = False
        self.final = []

    def _sem(self, key):
        if key not in self.sems:
            self.sems[key] = self.stack.enter_context(self.nc.semaphore(key))
            self.count[key] = 0
        return key

    def _waits(self, eng, reads, writes, tok_sem, is_dma):
        need = {}

        def add(t, kind):
            if t is None:
                return
            sem, val, teng = t
            if (not is_dma) and teng == eng:
                if eng == "pe":
                    return
            if is_dma and sem == tok_sem and kind == "waw":
                return
            if need.get(sem, 0) < val:
                need[sem] = val

        for b in reads:
            add(b.w, "raw")
        for b in writes:
            add(b.w, "waw")
            for sem, (val, teng) in b.rs.items():
                add((sem, val, teng), "war")
        waits = []
        for sem, val in need.items():
            if self.seen[eng].get(sem, 0) < val:
                self.seen[eng][sem] = val
                waits.append((sem, val))
        return waits

    def _update(self, tok, reads, writes):
        for b in reads:
            b.rs[tok[0]] = (tok[1], tok[2])
        for b in writes:
            b.w = tok
            b.rs = {}

    def op(self, eng, fn, reads=(), writes=()):
        if self.dry:
            return
        k = self.eng_key.get(eng)
        if k is None or self.count[k] >= 3000:
            k = f"{eng}_{self.eng_n[eng]}"
            self.eng_n[eng] += 1
            self._sem(k)
            self.eng_key[eng] = k
        waits = self._waits(eng, reads, writes, k, False)
        self.count[k] += 1
        tok = (k, self.count[k], eng)
        self.ops[eng].append(Op(waits, fn, (k, 1)))
        self._update(tok, reads, writes)
        return tok

    def dma(self, queue, chan, fn, reads=(), writes=(), inc=16):
        if self.dry:
            return
        k = self._sem("d_" + chan)
        waits = self._waits(queue, reads, writes, k, True)
        self.count[k] += inc
        tok = (k, self.count[k], "dma")
        self.ops[queue].append(Op(waits, fn, (k, inc)))
        self._update(tok, reads, writes)
        return tok

    def check_deadlock(self):
        cnt = {k: 0 for k in self.sems}
        ptr = {e: 0 for e in ENG}
        progress = True
        while progress:
            progress = False
            for e in ENG:
                while ptr[e] < len(self.ops[e]):
                    o = self.ops[e][ptr[e]]
                    if all(cnt[s] >= v for s, v in o.waits):
                        cnt[o.inc[0]] += o.inc[1]
                        ptr[e] += 1
                        progress = True
                    else:
                        break
        stuck = {e: (ptr[e], len(self.ops[e])) for e in ENG if ptr[e] < len(self.ops[e])}
        if stuck:
            msg = []
            for e, (p, n) in stuck.items():
                o = self.ops[e][p]
                msg.append(f"{e} stuck at op {p}/{n} waits={[(s, v, cnt[s]) for s, v in o.waits if cnt[s] < v]}")
            raise RuntimeError("semaphore deadlock: " + "; ".join(msg))
        for s, v in self.final:
            assert cnt[s] >= v

    def emit(self):
        nc = self.nc
        self.check_deadlock()
        with nc.Block() as block:
            def mk(name):
                def f(e):
                    for o in self.ops[name]:
                        for sem, val in o.waits:
                            e.wait_ge(self.sems[sem], val)
                        last = o.fn(e)
                        last.then_inc(self.sems[o.inc[0]], o.inc[1])
                    if name == "sp":
                        for sem, val in self.final:
                            e.wait_ge(self.sems[sem], val)
                return f
            block.tensor(mk("pe"))
            block.scalar(mk("act"))
            block.vector(mk("dve"))
            block.gpsimd(mk("pool"))
            block.sync(mk("sp"))


class WStream:
    def __init__(self, S, ring_tensor, nr):
        self.S = S
        self.ring = ring_tensor
        self.nr = nr
        self.bufs = [Buf(f"ring{i}") for i in range(nr)]
        self.srcs = []
        self.next_get = 0
        self.next_load = 0
        self.after = ()

    def view(self, i, kind):
        r = self.ring[:, i * 8192:(i + 1) * 8192]
        if kind == "A":
            return r.rearrange("p (k n) -> p k n", k=16)
        return r.rearrange("p (k n) -> p k n", k=4)

    def _issue(self, idx):
        src, kind = self.srcs[idx]
        i = idx % self.nr
        dst = self.view(i, kind)

        def fn(e, dst=dst, src=src):
            return e.dma_start(out=dst, in_=src)
        self.S.dma("pool", f"ring{i}", fn, reads=(self.after if idx in (1, 2) else ()), writes=(self.bufs[i],))

    def get(self, src, kind, live=0):
        if self.S.dry:
            self.srcs.append((src, kind))
            return None, None
        idx = self.next_get
        self.next_get += 1
        while self.next_load < min(len(self.srcs), idx + self.nr - live):
            self._issue(self.next_load)
            self.next_load += 1
        i = idx % self.nr
        return self.view(i, kind), self.bufs[i]


def wslabA(w, c0):
    return w.rearrange("(k p) n -> p k n", p=128)[:, :, c0:c0 + 512]


def wslabB(w, r0):
    return w[r0:r0 + 512, :].rearrange("(k p) n -> p k n", p=128)


def build(layers, do_mixer=True, do_ffn=True, final_ln_only=False):
    nc = bass.Bass("TRN2", target_bir_lowering=False)
    dram = {}

    def din(name, shape, dt=F32):
        dram[name] = nc.dram_tensor(name, list(shape), dt, kind="ExternalInput").ap()
        return dram[name]

    xT = din("xT", [D, T])
    ppd = din("pp", [128, NPP])
    cstd = din("cst", [128, NCS])
    outT = nc.dram_tensor("outT", [D, T], F32, kind="ExternalOutput").ap()
    W = {}
    for i in layers:
        kind = i % 3
        if do_mixer:
            if kind in (0, 1):
                W[i, "qkv"] = din(f"wqkv{i}", [D, 3 * D])
                W[i, "o"] = din(f"wo{i}", [D, D])
                if kind == 0:
                    W[i, "wf"] = din(f"wf{i}", [128, 16 * 48])
                else:
                    W[i, "relb"] = din(f"relb{i}", [H, 128, 640])
            else:
                W[i, "pw1"] = din(f"pw1{i}", [D, 2 * D])
                W[i, "pw2"] = din(f"pw2{i}", [D, D])
                W[i, "cp"] = din(f"cp{i}", [128, NCP])
        if do_ffn:
            W[i, "g"] = din(f"wg{i}", [D, F])
            W[i, "u"] = din(f"wu{i}", [D, F])
            W[i, "d"] = din(f"wd{i}", [F, D])

    def dscr(name, shape, dt):
        return nc.dram_tensor(name, list(shape), dt, kind="Internal").ap()

    kT_own = [dscr(f"kT_own{p}", [D // 2, T], BF16) for p in range(2)]
    kT_g = [dscr(f"kT_g{p}", [D, T], BF16) for p in range(2)]
    v_own = [dscr(f"v_own{p}", [T, D // 2], BF16) for p in range(2)]
    v_g = [dscr(f"v_g{p}", [2 * T, D // 2], BF16) for p in range(2)]
    c_own = dscr("c_own", [16, T], F32)
    c_g = dscr("c_g", [32, T], F32)
    halo_own = [dscr(f"halo_own{g}", [512, 32], BF16) for g in range(4)]
    halo_g = [dscr(f"halo_g{g}", [1024, 32], BF16) for g in range(4)]
    yscr = dscr("yscr", [D, T], F32)

    stack = contextlib.ExitStack()
    with stack:
        def sb(name, shape, dt):
            return stack.enter_context(nc.sbuf_tensor("s_" + name, list(shape), dt))
        xr = sb("xr", [128, FC, T], F32)
        xb = sb("xb", [128, FC, T], BF16)
        ring = sb("ring", [128, 3 * 8192], BF16)
        arena = sb("arena", [128, NBLK * 1024], F32)
        pp = sb("pp", [128, NPP], F32)
        ones_f = sb("ones_f", [128, 128], F32)
        ones_b = sb("ones_b", [128, 128], BF16)
        tri_b = sb("tri_b", [128, 128], BF16)
        ident_b = sb("ident_b", [128, 128], BF16)
        ident = sb("ident", [16, 16], F32)
        wf_t = sb("wf_t", [128, 16 * 48], BF16)
        kb = sb("kb", [128, 256], F32)
        rl = sb("rl", [128, 512], F32)
        nbf = sb("nbf", [48, 2], F32)
        ps = stack.enter_context(nc.psum_tensor("psum_all", [128, 8, 512], F32))

        S = Sched(nc, stack)
        ws = WStream(S, ring, 3)
        B_xr = [Buf(f"xr{i}") for i in range(FC)]
        B_xb = [Buf(f"xb{i}") for i in range(FC)]
        B_ps = [Buf(f"ps{i}") for i in range(8)]
        B_ar = [Buf(f"ar{i}") for i in range(NBLK)]
        B_pp, B_cst, B_kb, B_rl, B_wf, B_nbf = Buf("pp"), Buf("cst"), Buf("kb"), Buf("rl"), Buf("wf"), Buf("nbf")
        B_co, B_cg = (Buf(n) for n in ("co", "cg"))
        B_y = [Buf("yscr0"), Buf("yscr1")]
        B_kTg, B_vg = ([Buf(n + str(p)) for p in range(2)] for n in ("kTg", "vg"))
        B_kTo, B_vo = ([[Buf(f"{n}{p}_{j}") for j in range(2)] for p in range(2)] for n in ("kTo", "vo"))
        B_ho = [Buf(f"ho{g}") for g in range(4)]
        B_hg = [Buf(f"hg{g}") for g in range(4)]
        B_out = Buf("out")
        ws.after = (B_xr[FC - 1],)
        B_pT = [Buf(f"pT{i}") for i in range(4)]
        B_kt = [[Buf(f"kt{s}c"), Buf(f"kt{s}o")] for s in range(2)]
        B_vt = [[Buf(f"vt{s}c"), Buf(f"vt{s}o")] for s in range(2)]
        B_lacc = [Buf("lacc0"), Buf("lacc1")]
        B_stg = [Buf("stg0"), Buf("stg1")]
        B_ps5 = [Buf("ps5a"), Buf("ps5b")]
        B_ps7 = [Buf("ps7a"), Buf("ps7b")]
        B_rl2 = [Buf("rl2a"), Buf("rl2b")]

        def alias_sync(subs, blk):
            def absorb(dst, srcb):
                for k_, v_ in srcb.rs.items():
                    if dst.rs.get(k_, (0, None))[0] < v_[0]:
                        dst.rs[k_] = v_
                if srcb.w is not None:
                    k_, val_, eng_ = srcb.w
                    if dst.rs.get(k_, (0, None))[0] < val_:
                        dst.rs[k_] = (val_, eng_)
            for s_ in subs:
                absorb(blk, s_)
            for s_ in subs:
                absorb(s_, blk)

        def alias_sync_all():
            alias_sync(B_stg, B_ar[8])
            alias_sync(B_pT, B_ar[6])
            alias_sync(B_rl2, B_rl)
            for s_ in range(2):
                alias_sync(B_kt[s_], B_ar[2 + s_])
                alias_sync(B_vt[s_], B_ar[4 + s_])

        def AF32(b0, nb=1):
            return arena[:, b0 * 1024:(b0 + nb) * 1024]

        def ABF(b0, nb=1):
            return arena[:, b0 * 1024:(b0 + nb) * 1024].bitcast(BF16)

        def AB(b0, nb=1):
            return B_ar[b0:b0 + nb]

        def ppc(c, n=1, rows=128):
            return pp[0:rows, c:c + n]

        state = {"pair": 0, "first_acc": True}
        pairs_all = [(0, 1), (2, 3), (4, 5), (6, 7)]
        state["pairs"] = pairs_all

        def next_pair():
            p = state["pairs"][state["pair"] % len(state["pairs"])]
            state["pair"] += 1
            return p

        def ps2(pair):
            return ps[:, pair[0]:pair[0] + 2, :]

        def v2(ap):
            return ap.rearrange("p (a b) -> p a b", a=2)

        def proj_fm(slab, sbuf_, evac, nchs=4):
            if state.get("after_ln") and len(state["pairs"]) == 4:
                state["after_ln"] = False
                banks = pairs_all
                for kc in range(16):
                    def fk(pe, kc=kc):
                        last = None
                        for nch in range(4):
                            for tb in range(2):
                                last = pe.matmul(ps[:, banks[nch][tb], :], lhsT=slab[:, kc, nch * 128:(nch + 1) * 128],
                                                 rhs=xb[:, kc, tb * 512:(tb + 1) * 512], start=(kc == 0), stop=(kc == 15))
                        return last
                    S.op("pe", fk, reads=[sbuf_, B_xb[kc]], writes=B_ps)
                for nch in range(4):
                    evac(nch, banks[nch])
                return
            state["after_ln"] = False
            for nch in range(nchs):
                pair = next_pair()

                def fn(pe, nch=nch, pair=pair):
                    last = None
                    for kc in range(16):
                        for tb in range(2):
                            last = pe.matmul(ps[:, pair[tb], :], lhsT=slab[:, kc, nch * 128:(nch + 1) * 128],
                                             rhs=xb[:, kc, tb * 512:(tb + 1) * 512], start=(kc == 0), stop=(kc == 15))
                    return last
                S.op("pe", fn, reads=[sbuf_] + B_xb, writes=[B_ps[pair[0]], B_ps[pair[1]]])
                evac(nch, pair)

        def proj_acc(src, src_bufs, nk, slab, sbuf_, last=False, bias=None):
            first = state["first_acc"]
            state["first_acc"] = False
            s1, s2, sqb = AF32(12), AF32(11), AF32(10)
            for n in range(16):
                pair = next_pair()

                def fn(pe, n=n, pair=pair):
                    last_i = None
                    for kc in range(nk):
                        for tb in range(2):
                            last_i = pe.matmul(ps[:, pair[tb], :], lhsT=slab[:, kc, n * 128:(n + 1) * 128],
                                               rhs=src[:, kc, tb * 512:(tb + 1) * 512], start=(kc == 0), stop=(kc == nk - 1))
                    return last_i
                S.op("pe", fn, reads=[sbuf_] + list(src_bufs), writes=[B_ps[pair[0]], B_ps[pair[1]]])
                rd = [B_ps[pair[0]], B_ps[pair[1]], B_xr[n]]
                if first:
                    assert bias is None
                    def fa(e, n=n, pair=pair):
                        return e.scalar_tensor_tensor(out=v2(xr[:, n, :]), in0=v2(xr[:, n, :]), scalar=ALPHA,
                                                      in1=ps2(pair), op0=ALU.mult, op1=ALU.add)
                elif bias is not None:
                    bc, bb = bias
                    rd = rd + list(bb)
                    def fa(e, n=n, pair=pair, bc=bc):
                        return e.scalar_tensor_tensor(out=v2(xr[:, n, :]), in0=ps2(pair), scalar=bc(n),
                                                      in1=v2(xr[:, n, :]), op0=ALU.add, op1=ALU.add)
                else:
                    def fa(e, n=n, pair=pair):
                        return e.tensor_tensor(out=v2(xr[:, n, :]), in0=ps2(pair), in1=v2(xr[:, n, :]), op=ALU.add)
                S.op("dve", fa, reads=rd, writes=[B_xr[n]])
                if last:
                    if n == 0:
                        S.op("pool", lambda e, n=n: e.tensor_copy(out=s1, in_=xr[:, n, :]), reads=[B_xr[n]], writes=AB(12))
                    else:
                        S.op("pool", lambda e, n=n: e.tensor_tensor(out=s1, in0=s1, in1=xr[:, n, :], op=ALU.add),
                             reads=[B_xr[n]] + AB(12), writes=AB(12))
                    S.op("act", lambda e, n=n: e.activation(out=sqb, in_=xr[:, n, :], func=AF.Square), reads=[B_xr[n]], writes=AB(10))
                    if n == 0:
                        S.op("dve", lambda e: e.tensor_copy(out=s2, in_=sqb), reads=AB(10), writes=AB(11))
                    else:
                        S.op("dve", lambda e: e.tensor_tensor(out=s2, in0=s2, in1=sqb, op=ALU.add), reads=AB(10) + AB(11), writes=AB(11))

        def ln_stats_finish(s1, s2, b_s1, b_s2, mean, rstd, b_mean, b_rstd, tmp, b_tmp):
            pa, pb = next_pair(), next_pair()

            def f1(pe):
                last = None
                for tb in range(2):
                    last = pe.matmul(ps[:, pa[tb], :], lhsT=ones_f[:, :], rhs=s1[:, tb * 512:(tb + 1) * 512], start=True, stop=True)
                for tb in range(2):
                    last = pe.matmul(ps[:, pb[tb], :], lhsT=ones_f[:, :], rhs=s2[:, tb * 512:(tb + 1) * 512], start=True, stop=True)
                return last
            S.op("pe", f1, reads=[b_s1, b_s2, B_cst], writes=[B_ps[pa[0]], B_ps[pa[1]], B_ps[pb[0]], B_ps[pb[1]]])
            S.op("act", lambda e: e.activation(out=v2(mean), in_=ps2(pa), func=AF.Copy, scale=1.0 / D),
                 reads=[B_ps[pa[0]], B_ps[pa[1]]], writes=[b_mean])
            S.op("dve", lambda e: e.tensor_tensor(out=tmp, in0=mean, in1=mean, op=ALU.mult), reads=[b_mean], writes=[b_tmp])
            S.op("dve", lambda e: e.scalar_tensor_tensor(out=v2(rstd), in0=ps2(pb), scalar=1.0 / D, in1=v2(tmp),
                                                         op0=ALU.mult, op1=ALU.subtract),
                 reads=[B_ps[pb[0]], B_ps[pb[1]], b_tmp], writes=[b_rstd])
            S.op("act", lambda e: e.activation(out=rstd, in_=rstd, func=AF.Sqrt, bias=epsc[:, 0:1], scale=1.0),
                 reads=[b_rstd, B_cst], writes=[b_rstd])
            S.op("dve", lambda e: e.reciprocal(out=rstd, in_=rstd), reads=[b_rstd], writes=[b_rstd])

        def layer_norm(gcol, bcol, final=False):
            alias_sync_all()
            s1, s2, mean, rstd = AF32(12), AF32(11), AF32(0), AF32(1)
            t = [AF32(2), AF32(3)]
            yst = [AF32(4), AF32(5)]
            ln_stats_finish(s1, s2, B_ar[12], B_ar[11], mean, rstd, B_ar[0], B_ar[1], AF32(6), B_ar[6])
            t = [AF32(2), AF32(3), AF32(7), AF32(8)]
            tbuf = [AB(2), AB(3), AB(7), AB(8)]

            def ln_sub(fc):
                tt, bt = t[fc % 4], tbuf[fc % 4]
                S.op("dve", lambda e: e.tensor_tensor(out=tt, in0=xr[:, fc, :], in1=mean, op=ALU.subtract),
                     reads=[B_xr[fc], B_ar[0]], writes=bt)

            def ln_rest(fc):
                tt, bt = t[fc % 4], tbuf[fc % 4]
                S.op("dve", lambda e: e.tensor_tensor(out=tt, in0=tt, in1=rstd, op=ALU.mult), reads=bt + [B_ar[1]], writes=bt)
                if not final:
                    S.op("act", lambda e: e.activation(out=xb[:, fc, :], in_=tt, func=AF.Identity,
                                                       scale=ppc(gcol + fc), bias=ppc(bcol + fc)),
                         reads=bt + [B_pp], writes=[B_xb[fc]])
                    S.op("act", lambda e: e.activation(out=xr[:, fc, :], in_=tt, func=AF.Identity,
                                                       scale=ppc(gcol + fc), bias=ppc(bcol + fc)),
                         reads=bt + [B_pp], writes=[B_xr[fc]])
                else:
                    ys = yst[fc % 2]
                    by = AB(4 + fc % 2)
                    S.op("act", lambda e: e.activation(out=ys, in_=tt, func=AF.Identity,
                                                       scale=ppc(gcol + fc), bias=ppc(bcol + fc)),
                         reads=bt + [B_pp], writes=by)
                    tok = S.dma("sp", f"out{fc % 2}", lambda e: e.dma_start(out=outT[fc * 128:(fc + 1) * 128, :], in_=ys),
                                reads=by, writes=[B_out])
                    if tok is not None:
                        S.final.append((tok[0], tok[1]))
            for f2 in range(0, FC, 2):
                ln_sub(f2)
                ln_sub(f2 + 1)
                ln_rest(f2)
                ln_rest(f2 + 1)
            alias_sync_all()
            state["first_acc"] = True
            state["after_ln"] = True

        def ffn(i):
            hg = ABF(0, 2).rearrange("p (k t) -> p k t", k=4)
            hb = [ABF(2, 2).rearrange("p (k t) -> p k t", k=4), ABF(4, 2).rearrange("p (k t) -> p k t", k=4)]
            hbb = [AB(2, 2), AB(4, 2)]
            for g in range(NG):
                slab, sbuf_ = ws.get(wslabA(W[i, "g"], g * 512), "A")

                def ev_g(nch, pair):
                    S.op("act", lambda e: e.activation(out=v2(hg[:, nch, :]), in_=ps2(pair), func=AF.Silu),
                         reads=[B_ps[pair[0]], B_ps[pair[1]]], writes=AB(0, 2))
                if not S.dry:
                    proj_fm(slab, sbuf_, ev_g)
                slab, sbuf_ = ws.get(wslabA(W[i, "u"], g * 512), "A")
                hcur, hcb = hb[g % 2], hbb[g % 2]

                def ev_u(nch, pair, hcur=hcur, hcb=hcb):
                    S.op("dve", lambda e: e.tensor_tensor(out=v2(hcur[:, nch, :]), in0=ps2(pair), in1=v2(hg[:, nch, :]), op=ALU.mult),
                         reads=[B_ps[pair[0]], B_ps[pair[1]]] + AB(0, 2), writes=hcb)
                if not S.dry:
                    proj_fm(slab, sbuf_, ev_u)
                slab, sbuf_ = ws.get(wslabB(W[i, "d"], g * 512), "B")
                if not S.dry:
                    proj_acc(hcur, hcb, 4, slab, sbuf_, last=(g == NG - 1))

        def attn_phase_a(i, fox):
            wq = W[i, "qkv"]
            stg = [ABF(8)[:, 0:1024], ABF(8)[:, 1024:2048]]
            k = [0]

            def gather(name, p_, own, gat, b_own, b_gat):
                S.dma("pool", f"cc_{name}{p_}", lambda e: e.collective_compute("AllGather", ALU.bypass, replica_groups=GROUPS,
                                                                               ins=[own[:, :]], outs=[gat[:, :]]),
                      reads=b_own, writes=[b_gat], inc=1)

            for s in range(4):
                slab, sbuf_ = ws.get(wslabA(wq, D + s * 512), "A")

                def ev_k(nch, pair, s=s):
                    j = k[0] % 2
                    k[0] += 1
                    S.op("act", lambda e: e.activation(out=v2(stg[j]), in_=ps2(pair), func=AF.Copy),
                         reads=[B_ps[pair[0]], B_ps[pair[1]]], writes=[B_stg[j]])
                    r0 = (s * 4 + nch) * 128
                    hp, rr = r0 // 1024, r0 % 1024
                    S.dma("sp", f"stg{j}", lambda e: e.dma_start(out=kT_own[hp][rr:rr + 128, :], in_=stg[j]),
                          reads=[B_stg[j]], writes=[B_kTo[hp][j]])
                if not S.dry:
                    proj_fm(slab, sbuf_, ev_k)
                    if s % 2 == 1:
                        gather("k", s // 2, kT_own[s // 2], kT_g[s // 2], B_kTo[s // 2], B_kTg[s // 2])
            if fox and not S.dry:
                fox_forget(i)

            def v_tile(s, tt, slab, sbuf_):
                pair = next_pair()
                bank = pair[0]

                def fn(pe):
                    last = None
                    for kc in range(16):
                        last = pe.matmul(ps[:, bank, :], lhsT=xb[:, kc, tt * 128:(tt + 1) * 128], rhs=slab[:, kc, :],
                                         start=(kc == 0), stop=(kc == 15))
                    return last
                S.op("pe", fn, reads=[sbuf_] + B_xb, writes=[B_ps[bank]])
                j = k[0] % 2
                k[0] += 1
                if tt % 2 == 0:
                    S.op("act", lambda e: e.activation(out=stg[j][:, 0:512], in_=ps[:, bank, :], func=AF.Copy),
                         reads=[B_ps[bank]], writes=[B_stg[j]])
                else:
                    S.op("dve", lambda e: e.tensor_copy(out=stg[j][:, 0:512], in_=ps[:, bank, :]),
                         reads=[B_ps[bank]], writes=[B_stg[j]])
                hp, c0 = s // 2, (s % 2) * 512
                S.dma("sp", f"stg{j}", lambda e: e.dma_start(out=v_own[hp][tt * 128:(tt + 1) * 128, c0:c0 + 512], in_=stg[j][:, 0:512]),
                      reads=[B_stg[j]], writes=[B_vo[hp][j]])

            for s in range(4):
                slab, sbuf_ = ws.get(wslabA(wq, 2 * D + s * 512), "A")
                if S.dry:
                    continue
                for tt in range(8):
                    v_tile(s, tt, slab, sbuf_)
                if s % 2 == 1:
                    gather("v", s // 2, v_own[s // 2], v_g[s // 2], B_vo[s // 2], B_vg[s // 2])
            if fox and not S.dry:
                fox_ctx_bias()

        def fox_forget(i):
            j = i // 3
            cn, tmpe, ones48 = AF32(12), AF32(13), AF32(9)
            wfv = wf_t[:, :].rearrange("p (k m) -> p k m", k=16)
            S.dma("pool", "wf", lambda e: e.dma_start(out=wf_t[:, :], in_=W[i, "wf"]), reads=(), writes=[B_wf])
            pair = next_pair()

            def fz(pe):
                last = None
                for kc in range(16):
                    for tb in range(2):
                        last = pe.matmul(ps[0:48, pair[tb], :], lhsT=wfv[:, kc, :], rhs=xb[:, kc, tb * 512:(tb + 1) * 512],
                                         start=(kc == 0), stop=(kc == 15))
                return last
            S.op("pe", fz, reads=[B_wf] + B_xb, writes=[B_ps[pair[0]], B_ps[pair[1]]])
            S.op("act", lambda e: e.activation(out=v2(tmpe[0:48, :]), in_=ps[0:48, pair[0]:pair[0] + 2, :], func=AF.Exp,
                                               bias=nbf[0:48, j:j + 1], scale=-1.0),
                 reads=[B_ps[pair[0]], B_ps[pair[1]], B_nbf], writes=AB(13))
            S.op("act", lambda e: e.activation(out=tmpe[0:48, :], in_=tmpe[0:48, :], func=AF.Ln, bias=onec[0:48, 0:1], scale=1.0),
                 reads=AB(13) + [B_cst], writes=AB(13))
            S.op("pool", lambda e: e.memset(ones48[0:48, :], 1.0), reads=(), writes=AB(9))
            S.op("dve", lambda e: e.tensor_tensor_scan(out=cn[0:48, :], data0=ones48[0:48, :], data1=tmpe[0:48, :], initial=0.0,
                                                        op0=ALU.mult, op1=ALU.add),
                 reads=AB(9) + AB(13), writes=AB(12))
            S.dma("sp", "cown", lambda e: e.dma_start(out=c_own[:, :], in_=cn[0:16, :]), reads=AB(12), writes=[B_co])
            S.dma("pool", "cc_c", lambda e: e.collective_compute("AllGather", ALU.bypass, replica_groups=GROUPS,
                                                               ins=[c_own[:, :]], outs=[c_g[:, :]]),
                  reads=[B_co], writes=[B_cg], inc=1)
            hl = ABF(11)[:, 0:1024]
            hi32 = ABF(11)[:, 1024:2048]
            S.op("act", lambda e: e.activation(out=hl[0:48, :], in_=cn[0:48, :], func=AF.Copy, scale=-1.0),
                 reads=AB(12), writes=AB(11))
            S.op("act", lambda e: e.activation(out=hi32[32:48, :], in_=cn[32:48, :], func=AF.Copy, scale=-1.0),
                 reads=AB(12), writes=AB(11))
            S.op("dve", lambda e: e.scalar_tensor_tensor(out=hl[32:48, :], in0=cn[32:48, :], scalar=-1.0, in1=hi32[32:48, :],
                                                         op0=ALU.mult, op1=ALU.subtract),
                 reads=AB(12) + AB(11), writes=AB(11))
            pr = next_pair()

            def ft(pe):
                last = None
                for t_ in range(8):
                    last = pe.matmul(ps[:, pr[0], t_ * 16:(t_ + 1) * 16], lhsT=cn[0:16, t_ * 128:(t_ + 1) * 128], rhs=ident[:, :],
                                     start=True, stop=True)
                return last
            S.op("pe", ft, reads=AB(12) + [B_cst], writes=[B_ps[pr[0]]])
            S.op("dve", lambda e: e.tensor_copy(out=kb[:, 0:128], in_=ps[:, pr[0], 0:128]), reads=[B_ps[pr[0]]], writes=[B_kb])

        def fox_ctx_bias():
            cctx = AF32(13)
            S.dma("sp", "cctx", lambda e: e.dma_start(out=cctx[0:16, :], in_=c_g[0:16, :]), reads=[B_cg], writes=AB(13))
            S.op("dve", lambda e: e.tensor_scalar(out=cctx[0:16, :], in0=cctx[0:16, :], scalar1=cctx[0:16, 1023:1024], scalar2=None,
                                                  op0=ALU.subtract),
                 reads=AB(13), writes=AB(13))
            pr2 = next_pair()

            def ft2(pe):
                last = None
                for t_ in range(8):
                    last = pe.matmul(ps[:, pr2[0], t_ * 16:(t_ + 1) * 16], lhsT=cctx[0:16, t_ * 128:(t_ + 1) * 128], rhs=ident[:, :],
                                     start=True, stop=True)
                return last
            S.op("pe", ft2, reads=AB(13) + [B_cst], writes=[B_ps[pr2[0]]])
            S.op("dve", lambda e: e.tensor_scalar(out=kb[:, 128:256], in0=ps[:, pr2[0], 0:128], scalar1=ppc(PP_CTXMASK), scalar2=None,
                                                  op0=ALU.add),
                 reads=[B_ps[pr2[0]], B_pp], writes=[B_kb])

        def load_kv(h, slot, ctx_lo):
            kt = ABF(2 + slot)
            vt = ABF(4 + slot).rearrange("p (t d) -> p t d", t=16)
            hp, rr = h // 8, (h % 8) * 128
            t0 = ctx_lo // 128
            S.dma("sp", f"kt{slot}c", lambda e: e.dma_start(out=kt[:, ctx_lo:1024], in_=kT_g[hp][rr:rr + 128, ctx_lo:1024]),
                  reads=[B_kTg[hp]], writes=[B_kt[slot][0]])
            S.dma("sp", f"kt{slot}o", lambda e: e.dma_start(out=kt[:, 1024:2048], in_=kT_own[hp][rr:rr + 128, :]),
                  reads=B_kTo[hp], writes=[B_kt[slot][1]])
            S.dma("sp", f"vt{slot}c", lambda e: e.dma_start(
                out=vt[:, t0:8, :], in_=v_g[hp][ctx_lo:1024, rr:rr + 128].rearrange("(t p) d -> p t d", p=128)),
                reads=[B_vg[hp]], writes=[B_vt[slot][0]])
            S.dma("sp", f"vt{slot}o", lambda e: e.dma_start(
                out=vt[:, 8:16, :], in_=v_own[hp][:, rr:rr + 128].rearrange("(t p) d -> p t d", p=128)),
                reads=B_vo[hp], writes=[B_vt[slot][1]])
            return kt, vt

        def q_proj(i, hgp, qT):
            slab, sbuf_ = ws.get(wslabA(W[i, "qkv"], hgp * 512), "A")

            def ev_q(nch, pair):
                S.op("act", lambda e: e.activation(out=v2(qT[:, nch, :]), in_=ps2(pair), func=AF.Copy, scale=QSCALE),
                     reads=[B_ps[pair[0]], B_ps[pair[1]]], writes=AB(0, 2))
            if not S.dry:
                proj_fm(slab, sbuf_, ev_q)

        def fox_layer(i):
            attn_phase_a(i, True)
            qT = ABF(0, 2).rearrange("p (k t) -> p k t", k=4)
            ob = ABF(8, 2).rearrange("p (k t) -> p k t", k=4)
            pT = [ABF(6)[:, s * 512:(s + 1) * 512] for s in range(4)]
            selv = ABF(10).rearrange("p (h m) -> p h m", h=16)
            hl = ABF(11)[:, 0:1024]
            state["pairs"] = [(0, 1), (2, 3)]
            kvs = {}
            if not S.dry:
                S.dma("pool", "sel", lambda e: e.dma_start(out=ABF(10)[0:48, :], in_=cstd[0:48, CS_SEL:CS_SEL + 2048]),
                      reads=(), writes=AB(10))
                kvs[0] = load_kv(0, 0, 0)

            qbc = [0]

            def qblock(h, hh, kt, vt, slot, qb):
                bo, bl = (6, 7) if qbc[0] % 2 == 0 else (2, 3)
                qbc[0] += 1
                tiles = [(j, 0, False) for j in range(8)]
                for jo in range(4 * qb + 4):
                    off = max(0, (jo - 4 * qb) * 128)
                    tiles.append((8 + jo, off, jo >= 4 * qb))
                nt = len(tiles)
                q0 = qb * 512

                def qk(idx):
                    j, off, diag = tiles[idx]
                    bank = 4 + idx % 2

                    def fn(pe):
                        pe.matmul(ps[:, bank, off:512], lhsT=kt[:, j * 128:(j + 1) * 128], rhs=qT[:, hh, q0 + off:q0 + 512],
                                  start=True, stop=False, skip_group_check=True)
                        return pe.matmul(ps[:, bank, off:512], lhsT=selv[0:48, h, :], rhs=hl[0:48, q0 + off:q0 + 512],
                                         start=False, stop=True, skip_group_check=True)
                    S.op("pe", fn, reads=B_kt[slot] + AB(0, 2) + AB(10) + AB(11), writes=[B_ps[bank]])
                    col = (128 + j * 16 + h) if j < 8 else ((j - 8) * 16 + h)
                    pt = pT[idx % 4]
                    bpt = [B_pT[idx % 4]]
                    S.op("act", lambda e: e.activation(out=pt[:, off:512], in_=ps[:, bank, off:512], func=AF.Exp,
                                                       bias=kb[:, col:col + 1], scale=1.0),
                         reads=[B_ps[bank], B_kb], writes=bpt)
                    if diag:
                        S.op("dve", lambda e: e.tensor_tensor(out=pt[:, off:off + 128], in0=pt[:, off:off + 128], in1=tri_b[:, :],
                                                              op=ALU.mult),
                             reads=bpt + [B_cst], writes=bpt)

                def pv(idx):
                    j, off, diag = tiles[idx]
                    pt = pT[idx % 4]

                    def fn(pe):
                        pe.matmul(ps[:, bo, off:512], lhsT=vt[:, j, :], rhs=pt[:, off:512], start=(idx == 0), stop=(idx == nt - 1),
                                  skip_group_check=True)
                        return pe.matmul(ps[:, bl, off:512], lhsT=ones_b[:, :], rhs=pt[:, off:512], start=(idx == 0),
                                         stop=(idx == nt - 1), skip_group_check=True)
                    S.op("pe", fn, reads=B_vt[slot] + [B_pT[idx % 4], B_cst], writes=[B_ps[bo], B_ps[bl]])
                qk(0)
                for idx in range(nt):
                    if idx + 1 < nt:
                        qk(idx + 1)
                    pv(idx)
                S.op("dve", lambda e: e.reciprocal(out=rl[:, :], in_=ps[:, bl, :]), reads=[B_ps[bl]], writes=[B_rl])
                S.op("dve", lambda e: e.tensor_tensor(out=ob[:, hh, q0:q0 + 512], in0=ps[:, bo, :], in1=rl[:, :], op=ALU.mult),
                     reads=[B_ps[bo], B_rl], writes=AB(8, 2))

            for hgp in range(4):
                q_proj(i, hgp, qT)
                for hh in range(4):
                    if S.dry:
                        continue
                    h = hgp * 4 + hh
                    kt, vt = kvs[h]
                    if h + 1 < 16:
                        kvs[h + 1] = load_kv(h + 1, (h + 1) % 2, 0)
                    for qb in range(2):
                        qblock(h, hh, kt, vt, h % 2, qb)
                slab, sbuf_ = ws.get(wslabB(W[i, "o"], hgp * 512), "B")
                if not S.dry:
                    proj_acc(ob, AB(8, 2), 4, slab, sbuf_, last=(hgp == 3))
            state["pairs"] = pairs_all

        def rel_layer(i):
            attn_phase_a(i, False)
            qT = ABF(0, 2).rearrange("p (k t) -> p k t", k=4)
            ob = ABF(8, 2).rearrange("p (k t) -> p k t", k=4)
            pT5 = [ABF(6)[:, 0:640], ABF(6)[:, 1024:1664]]
            sbt = [AF32(7)[:, 0:640], AF32(11)[:, 0:640]]
            sbb = [AB(7), AB(11)]
            relt = [AF32(12)[:, 0:640], AF32(13)[:, 0:640]]
            state["pairs"] = [(0, 1), (2, 3)]
            sc = [(4, 5), (6, 7)]
            kvs = {}
            if not S.dry:
                kvs[0] = load_kv(0, 0, 512)
                S.dma("sp", "relb0", lambda e: e.dma_start(out=relt[0], in_=W[i, "relb"][0, :, :]), reads=(), writes=AB(12))
            cnt = [0]

            def qtile(h, hh, kt, vt, slot, qi):
                c = cnt[0] % 2
                cnt[0] += 1
                bA, bB = sc[c]
                bO = 2 * c
                cB = 0
                rt = relt[slot]
                rtb = AB(12 + slot)
                nctx = max(0, 4 - qi)
                ktiles = [8 + qi - 4 + r for r in range(5)]
                sbc, sbcb = sbt[c], sbb[c]
                pt = pT5[c]
                bpt = [B_pT[c]]
                oc = 0

                def qk_part():
                    def fqk(pe):
                        last = None
                        for r in range(5):
                            dst = ps[:, bA, r * 128:(r + 1) * 128] if r < 4 else ps[:, bB, cB:cB + 128]
                            j = ktiles[r]
                            last = pe.matmul(dst, lhsT=kt[:, j * 128:(j + 1) * 128], rhs=qT[:, hh, qi * 128:(qi + 1) * 128],
                                             start=True, stop=True, skip_group_check=True)
                        return last
                    S.op("pe", fqk, reads=B_kt[slot] + AB(0, 2), writes=[B_ps[bA], B_ps[bB]])
                    flat = ps[:, bA:bA + 2, :].rearrange("p a b -> p (a b)")
                    S.op("dve", lambda e: e.tensor_tensor(out=sbc[:, 0:640], in0=flat[:, 0:640], in1=rt[:, 0:640], op=ALU.add),
                         reads=[B_ps[bA], B_ps[bB]] + rtb, writes=sbcb)
                    nc_ = nctx * 128
                    if nctx > 0:
                        S.op("act", lambda e: e.activation(out=pt[:, 0:nc_], in_=sbc[:, 0:nc_], func=AF.Exp, bias=ppc(PP_CTXMASK), scale=1.0),
                             reads=sbcb + [B_pp], writes=bpt)
                    S.op("act", lambda e: e.activation(out=pt[:, nc_:640], in_=sbc[:, nc_:640], func=AF.Exp), reads=sbcb, writes=bpt)

                def pv_part():
                    def fpv(pe):
                        last = None
                        for r in range(5):
                            j = ktiles[r]
                            pe.matmul(ps[:, bO, oc:oc + 128], lhsT=vt[:, j, :], rhs=pt[:, r * 128:(r + 1) * 128],
                                      start=(r == 0), stop=(r == 4), skip_group_check=True)
                        for r in range(5):
                            last = pe.matmul(ps[:, bO, oc + 128:oc + 256], lhsT=ones_b[:, :], rhs=pt[:, r * 128:(r + 1) * 128],
                                             start=(r == 0), stop=(r == 4), skip_group_check=True)
                        return last
                    S.op("pe", fpv, reads=B_vt[slot] + bpt + [B_cst], writes=[B_ps[bO]])
                    rc = c * 128
                    S.op("dve", lambda e: e.reciprocal(out=rl[:, rc:rc + 128], in_=ps[:, bO, oc + 128:oc + 256]),
                         reads=[B_ps[bO]], writes=[B_rl2[c]])
                    S.op("dve", lambda e: e.tensor_tensor(out=ob[:, hh, qi * 128:(qi + 1) * 128], in0=ps[:, bO, oc:oc + 128],
                                                          in1=rl[:, rc:rc + 128], op=ALU.mult),
                         reads=[B_ps[bO], B_rl2[c]], writes=AB(8, 2))
                return qk_part, pv_part

            def relb_load(hn):
                S.dma("sp", f"relb{hn % 2}", lambda e: e.dma_start(out=relt[hn % 2], in_=W[i, "relb"][hn, :, :]),
                      reads=(), writes=AB(12 + hn % 2))

            if not S.dry:
                kvs[1] = load_kv(1, 1, 512)
                relb_load(1)
            for hgp in range(4):
                q_proj(i, hgp, qT)
                if not S.dry:
                    steps = []
                    for hh in range(4):
                        h = hgp * 4 + hh
                        kt = ABF(2 + h % 2)
                        vt = ABF(4 + h % 2).rearrange("p (t d) -> p t d", t=16)
                        for qi in range(8):
                            steps.append((h, qi) + qtile(h, hh, kt, vt, h % 2, qi))
                    skew = not DEBUG.get("noskew")
                    if skew:
                        steps[0][2]()
                    for t_, (h, qi, qk_, pv_) in enumerate(steps):
                        if not skew:
                            qk_()
                        elif t_ + 1 < len(steps):
                            steps[t_ + 1][2]()
                        pv_()
                        if qi == 7 and h + 2 < 16:
                            kvs[h + 2] = load_kv(h + 2, h % 2, 512)
                            relb_load(h + 2)
                slab, sbuf_ = ws.get(wslabB(W[i, "o"], hgp * 512), "B")
                if not S.dry:
                    proj_acc(ob, AB(8, 2), 4, slab, sbuf_, last=(hgp == 3))
            state["pairs"] = pairs_all

        def conv_layer(i):
            cpt = AF32(13)[:, 0:NCP]
            ub = ABF(0, 3)[:, 0:4 * 1056].rearrange("p (k t) -> p k t", k=4)
            sg = [AF32(3), AF32(4)]
            dg = [ABF(5, 2)[:, 0:31 * 128].rearrange("p (k m) -> p k m", k=31), ABF(7, 2)[:, 0:31 * 128].rearrange("p (k m) -> p k m", k=31)]
            dgb = [AB(5, 2), AB(7, 2)]
            acc, sq, s1c, s2c = AF32(9), AF32(10), AF32(11), AF32(12)
            hal = rl[:, 0:64].bitcast(BF16).rearrange("p (k t) -> p k t", k=4)
            if not S.dry:
                S.dma("sp", "cp", lambda e: e.dma_start(out=cpt, in_=W[i, "cp"]), reads=(), writes=AB(13))

            gk = [0]

            def glu_unit(g, nch, tb, slab_g, sb_g, slab_a, sb_a):
                state["after_ln"] = False
                fcg = g * 4 + nch
                pair = next_pair()
                j = gk[0] % 2
                gk[0] += 1

                def fn(pe):
                    last = None
                    for which, slab in ((0, slab_g), (1, slab_a)):
                        for kc in range(16):
                            last = pe.matmul(ps[:, pair[which], :], lhsT=slab[:, kc, nch * 128:(nch + 1) * 128],
                                             rhs=xb[:, kc, tb * 512:(tb + 1) * 512], start=(kc == 0), stop=(kc == 15))
                    return last
                S.op("pe", fn, reads=[sb_g, sb_a] + B_xb, writes=[B_ps[pair[0]], B_ps[pair[1]]])
                S.op("act", lambda e: e.activation(out=sg[j][:, 0:512], in_=ps[:, pair[0], :], func=AF.Sigmoid,
                                                   bias=cpt[:, CP_BPW1 + 16 + fcg:CP_BPW1 + 17 + fcg], scale=1.0),
                     reads=[B_ps[pair[0]]] + AB(13), writes=AB(3 + j))
                c0 = 32 + tb * 512
                S.op("dve", lambda e: e.scalar_tensor_tensor(
                    out=ub[:, nch, c0:c0 + 512], in0=ps[:, pair[1], :], scalar=cpt[:, CP_BPW1 + fcg:CP_BPW1 + fcg + 1],
                    in1=sg[j][:, 0:512], op0=ALU.add, op1=ALU.mult),
                    reads=[B_ps[pair[1]]] + AB(13) + AB(3 + j), writes=AB(0, 3))

            def halo_send(g):
                S.dma("sp", "halo", lambda e: e.dma_start(out=halo_own[g].rearrange("(k p) t -> p k t", p=128), in_=ub[:, :, 1024:1056]),
                      reads=AB(0, 3), writes=[B_ho[g]])
                S.dma("pool", f"cc_h{g}", lambda e: e.collective_compute("AllGather", ALU.bypass, replica_groups=GROUPS,
                                                                         ins=[halo_own[g][:, :]], outs=[halo_g[g][:, :]]),
                      reads=[B_ho[g]], writes=[B_hg[g]], inc=1)

            def halo_recv(g):
                S.dma("sp", "halo", lambda e: e.dma_start(out=hal, in_=halo_g[g][0:512, :].rearrange("(k p) t -> p k t", p=128)),
                      reads=[B_hg[g]], writes=[B_rl])
                S.op("dve", lambda e: e.tensor_scalar(out=ub[:, :, 0:32], in0=hal, scalar1=ppc(PP_HASPREV), scalar2=None, op0=ALU.mult),
                     reads=[B_rl, B_pp], writes=AB(0, 3))

            conv_pairs = [(0, 1), (2, 3), (4, 5), (6, 7)]

            def diag_build(g, nch):
                fcg = g * 4 + nch
                wc = CP_WDW + fcg * 31
                d_, db_ = dg[nch % 2], dgb[nch % 2]

                def fdiag(e):
                    last = None
                    for k_ in range(31):
                        last = e.tensor_scalar(out=d_[:, k_, :], in0=ident_b[:, :], scalar1=cpt[:, wc + k_:wc + k_ + 1], scalar2=None,
                                               op0=ALU.mult)
                    return last
                S.op("dve", fdiag, reads=AB(13) + [B_cst], writes=db_)

            def conv_mm(g, nch, tb):
                d_, db_ = dg[nch % 2], dgb[nch % 2]
                bank = conv_pairs[nch][tb]

                def fconv(pe):
                    last = None
                    for k_ in range(31):
                        c0 = 2 + k_ + tb * 512
                        last = pe.matmul(ps[:, bank, :], lhsT=d_[:, k_, :], rhs=ub[:, nch, c0:c0 + 512],
                                         start=(k_ == 0), stop=(k_ == 30))
                    return last
                S.op("pe", fconv, reads=db_ + AB(0, 3), writes=[B_ps[bank]])

            def conv_evac(g, nch):
                fcg = g * 4 + nch
                pair = conv_pairs[nch]
                S.op("act", lambda e: e.activation(out=v2(acc), in_=ps2(pair), func=AF.Identity,
                                                   bias=cpt[:, CP_BDW + fcg:CP_BDW + fcg + 1], scale=1.0),
                     reads=[B_ps[pair[0]], B_ps[pair[1]]] + AB(13), writes=AB(9))
                first = (g == 0 and nch == 0)
                if first:
                    S.op("dve", lambda e: e.tensor_copy(out=s1c, in_=acc), reads=AB(9), writes=AB(11))
                else:
                    S.op("dve", lambda e: e.tensor_tensor(out=s1c, in0=s1c, in1=acc, op=ALU.add), reads=AB(9) + AB(11), writes=AB(11))
                S.op("act", lambda e: e.activation(out=sq, in_=acc, func=AF.Square), reads=AB(9), writes=AB(10))
                if first:
                    S.op("dve", lambda e: e.tensor_copy(out=s2c, in_=sq), reads=AB(10), writes=AB(12))
                else:
                    S.op("dve", lambda e: e.tensor_tensor(out=s2c, in0=s2c, in1=sq, op=ALU.add), reads=AB(10) + AB(12), writes=AB(12))
                S.dma("sp", "ysc", lambda e: e.dma_start(out=yscr[fcg * 128:(fcg + 1) * 128, :], in_=acc),
                      reads=AB(9), writes=[B_y[0]])

            for g in range(4):
                slab_g, sb_g = ws.get(wslabA(W[i, "pw1"], D + g * 512), "A")
                slab_a, sb_a = ws.get(wslabA(W[i, "pw1"], g * 512), "A", live=1)
                if S.dry:
                    continue
                for tb in (1, 0):
                    for nch in range(4):
                        glu_unit(g, nch, tb, slab_g, sb_g, slab_a, sb_a)
                    if tb == 1:
                        halo_send(g)
                for cp_ in range(2):
                    n0, n1 = 2 * cp_, 2 * cp_ + 1
                    diag_build(g, n0)
                    diag_build(g, n1)
                    conv_mm(g, n0, 1)
                    conv_mm(g, n1, 1)
                    if cp_ == 0:
                        halo_recv(g)
                    conv_mm(g, n0, 0)
                    conv_evac(g, n0)
                    conv_mm(g, n1, 0)
                    conv_evac(g, n1)
            mean, rstd = AF32(8), AF32(9)
            if DEBUG.get("conv_y"):
                if not S.dry:
                    tok = S.dma("sp", "dbg", lambda e: e.dma_start(out=outT[:, :], in_=yscr[:, :]), reads=B_y, writes=[B_out])
                    S.final.append((tok[0], tok[1]))
                return
            if not S.dry:
                ln_stats_finish(s1c, s2c, B_ar[11], B_ar[12], mean, rstd, B_ar[8], B_ar[9], AF32(10), B_ar[10])
            yb = AF32(0, 4).rearrange("p (k t) -> p k t", k=4)
            tt_ = [AF32(4), AF32(5)]
            hbvs = [(ABF(6, 2).rearrange("p (k t) -> p k t", k=4), AB(6, 2)), (ABF(10, 2).rearrange("p (k t) -> p k t", k=4), AB(10, 2))]

            def norm_chunk(g, nch):
                hbv, hbb = hbvs[1 if g == 1 else 0]
                fcg = g * 4 + nch
                t1 = tt_[nch % 2]
                tb_ = AB(4 + nch % 2)
                S.op("dve", lambda e: e.tensor_tensor(out=t1, in0=yb[:, nch, :], in1=mean, op=ALU.subtract),
                     reads=AB(0, 4) + AB(8), writes=tb_)
                S.op("dve", lambda e: e.tensor_tensor(out=t1, in0=t1, in1=rstd, op=ALU.mult), reads=tb_ + AB(9), writes=tb_)
                S.op("act", lambda e: e.activation(out=hbv[:, nch, :], in_=t1, func=AF.Silu,
                                                   scale=cpt[:, CP_LNG + fcg:CP_LNG + fcg + 1],
                                                   bias=cpt[:, CP_LNB + fcg:CP_LNB + fcg + 1]),
                     reads=tb_ + AB(13), writes=hbb)

            def yload(g):
                S.dma("sp", "yld", lambda e: e.dma_start(out=yb, in_=yscr[g * 512:(g + 1) * 512, :].rearrange("(k p) t -> p k t", p=128)),
                      reads=B_y, writes=AB(0, 4))

            for g in range(4):
                slab, sbuf_ = ws.get(wslabB(W[i, "pw2"], g * 512), "B")
                if S.dry:
                    continue
                yload(g)
                for nch in range(4):
                    norm_chunk(g, nch)
                hbv, hbb = hbvs[1 if g == 1 else 0]
                if g == 0:
                    proj_acc(hbv, hbb, 4, slab, sbuf_)
                    for n in range(16):
                        S.op("dve", lambda e, n=n: e.tensor_scalar(out=xr[:, n, :], in0=xr[:, n, :],
                                                                   scalar1=cpt[:, CP_BPW2 + n:CP_BPW2 + n + 1], scalar2=None, op0=ALU.add),
                             reads=[B_xr[n]] + AB(13), writes=[B_xr[n]])
                else:
                    proj_acc(hbv, hbb, 4, slab, sbuf_, last=(g == 3))

        epsc = sb("epsc", [128, 1], F32)
        onec = sb("onec", [128, 1], F32)

        def prologue():
            S.dma("sp", "pp", lambda e: e.dma_start(out=pp[:, :], in_=ppd[:, :]), reads=(), writes=[B_pp])
            S.dma("sp", "cst", lambda e: e.dma_start(out=ones_f[:, :], in_=cstd[:, CS_ONES:CS_ONES + 128]), reads=(), writes=[B_cst])
            S.dma("sp", "cst2", lambda e: e.dma_start(out=ident[:, :], in_=cstd[0:16, CS_ID:CS_ID + 16]), reads=(), writes=[B_cst])
            S.dma("pool", "cstb", lambda e: e.dma_start(out=ones_b[:, :], in_=cstd[:, CS_ONES:CS_ONES + 128]), reads=(), writes=[B_cst])
            S.dma("pool", "cstb2", lambda e: e.dma_start(out=tri_b[:, :], in_=cstd[:, CS_TRI:CS_TRI + 128]), reads=(), writes=[B_cst])
            S.dma("pool", "cstb3", lambda e: e.dma_start(out=ident_b[:, :], in_=cstd[:, CS_IDB:CS_IDB + 128]), reads=(), writes=[B_cst])
            S.op("pool", lambda e: e.memset(epsc[:, :], LN_EPS), reads=(), writes=[B_cst])
            S.op("pool", lambda e: e.memset(onec[:, :], 1.0), reads=(), writes=[B_cst])
            S.op("dve", lambda e: e.tensor_scalar(out=nbf[:, :], in0=pp[0:48, PP_BF:PP_BF + 2], scalar1=-1.0, scalar2=None, op0=ALU.mult),
                 reads=[B_pp], writes=[B_nbf])
            for fc in range(FC):
                S.dma("sp", f"xin{fc}", lambda e, fc=fc: e.dma_start(out=xr[:, fc, :], in_=xT[fc * 128:(fc + 1) * 128, :]),
                      reads=(), writes=[B_xr[fc]])
                S.op("act" if fc % 2 == 0 else "dve",
                     (lambda e, fc=fc: e.activation(out=xb[:, fc, :], in_=xr[:, fc, :], func=AF.Copy)) if fc % 2 == 0 else
                     (lambda e, fc=fc: e.tensor_copy(out=xb[:, fc, :], in_=xr[:, fc, :])),
                     reads=[B_xr[fc]], writes=[B_xb[fc]])

        def model():
            if not S.dry:
                prologue()
                state["after_ln"] = True
            last = layers[-1]
            for i in layers:
                kind = i % 3
                if do_mixer:
                    if kind == 0:
                        fox_layer(i)
                    elif kind == 1:
                        rel_layer(i)
                    else:
                        conv_layer(i)
                        if DEBUG.get("conv_y"):
                            break
                    if not S.dry:
                        layer_norm(PP_LN + i * 64, PP_LN + i * 64 + 16, final=(not do_ffn and i == last))
                if do_ffn:
                    ffn(i)
                    if not S.dry:
                        layer_norm(PP_LN + i * 64 + 32, PP_LN + i * 64 + 48, final=(i == last))

        S.dry = True
        model()
        S.dry = False
        state["pair"] = 0
        state["first_acc"] = True
        model()
        S.emit()
    return nc


def _cols(v):
    v = np.asarray(v, np.float32)
    return np.ascontiguousarray(v.reshape(-1, 128).T)


def _consts():
    c = np.zeros((128, NCS), np.float32)
    c[:, CS_ONES:CS_ONES + 128] = 1.0
    k = np.arange(128)[:, None]
    q = np.arange(128)[None, :]
    c[:, CS_TRI:CS_TRI + 128] = (q >= k).astype(np.float32)
    c[0:16, CS_ID:CS_ID + 16] = np.eye(16, dtype=np.float32)
    c[:, CS_IDB:CS_IDB + 128] = np.eye(128, dtype=np.float32)
    for h in range(16):
        c[h, CS_SEL + h * 128:CS_SEL + (h + 1) * 128] = 1.0
        c[32 + h, CS_SEL + h * 128:CS_SEL + (h + 1) * 128] = 1.0
    return c


def _relb_table(rel_bias):
    rb = np.asarray(rel_bias, np.float32)
    r = np.arange(5)[:, None, None]
    p = np.arange(128)[None, :, None]
    j = np.arange(128)[None, None, :]
    kpos = (r - 4) * 128 + p
    kch = np.floor_divide(kpos, 64)
    qch = j // 64
    valid = (kch >= qch - 8) & (kch <= qch)
    idx = np.clip(j - kpos, -128, 128) + 128
    idx, valid = np.broadcast_arrays(idx, valid)
    tab = rb[:, idx]
    tab = np.where(valid[None], tab, np.float32(NEG)).astype(np.float32)
    return np.ascontiguousarray(tab.transpose(0, 2, 1, 3).reshape(H, 128, 640))


def make_in_maps(inp, layers, do_mixer=True, do_ffn=True):
    x = np.asarray(inp["x"], np.float32)
    shared = {"cst": _consts()}
    pp = np.zeros((128, NPP), np.float32)
    for i in range(DEPTH):
        b = PP_LN + i * 64
        pp[:, b:b + 16] = _cols(inp["ln_mix_g"][i])
        pp[:, b + 16:b + 32] = _cols(inp["ln_mix_b"][i])
        pp[:, b + 32:b + 48] = _cols(inp["ln_ffn_g"][i])
        pp[:, b + 48:b + 64] = _cols(inp["ln_ffn_b"][i])
    for j in range(2):
        bf = np.asarray(inp["fox_b_f"][j], np.float32)
        pp[0:16, PP_BF + j] = bf
        pp[32:48, PP_BF + j] = bf
    for i in layers:
        kind, j = i % 3, i // 3
        if do_mixer:
            if kind == 0:
                shared[f"wqkv{i}"] = np.asarray(inp["fox_w_qkv"][j], np.float32)
                shared[f"wo{i}"] = np.asarray(inp["fox_w_o"][j], np.float32)
                wf = np.asarray(inp["fox_w_f"][j], np.float32)
                wfe = np.zeros((D, 48), np.float32)
                wfe[:, 0:16] = wf
                wfe[:, 32:48] = wf
                shared[f"wf{i}"] = np.ascontiguousarray(wfe.reshape(16, 128, 48).transpose(1, 0, 2).reshape(128, 16 * 48))
            elif kind == 1:
                shared[f"wqkv{i}"] = np.asarray(inp["rel_w_qkv"][j], np.float32)
                shared[f"wo{i}"] = np.asarray(inp["rel_w_o"][j], np.float32)
                shared[f"relb{i}"] = _relb_table(inp["rel_bias"][j])
            else:
                shared[f"pw1{i}"] = np.asarray(inp["conv_w_pw1"][j], np.float32)
                shared[f"pw2{i}"] = np.asarray(inp["conv_w_pw2"][j], np.float32)
                cp = np.zeros((128, NCP), np.float32)
                cp[:, CP_BPW1:CP_BPW1 + 32] = _cols(inp["conv_b_pw1"][j])
                wdw = np.asarray(inp["conv_w_dw"][j], np.float32)
                cp[:, CP_WDW:CP_WDW + 496] = wdw.reshape(31, 16, 128).transpose(2, 1, 0).reshape(128, 496)
                cp[:, CP_BDW:CP_BDW + 16] = _cols(inp["conv_b_dw"][j])
                cp[:, CP_LNG:CP_LNG + 16] = _cols(inp["conv_ln_g"][j])
                cp[:, CP_LNB:CP_LNB + 16] = _cols(inp["conv_ln_b"][j])
                cp[:, CP_BPW2:CP_BPW2 + 16] = _cols(inp["conv_b_pw2"][j])
                shared[f"cp{i}"] = cp
        if do_ffn:
            shared[f"wg{i}"] = np.asarray(inp["ffn_w_gate"][i], np.float32)
            shared[f"wu{i}"] = np.asarray(inp["ffn_w_up"][i], np.float32)
            shared[f"wd{i}"] = np.asarray(inp["ffn_w_down"][i], np.float32)
    maps = []
    for c in range(NCORES):
        b, hf = c // 2, c % 2
        m = dict(shared)
        m["xT"] = np.ascontiguousarray(x[b, hf * T:(hf + 1) * T, :].T)
        ppc_ = pp.copy()
        ppc_[:, PP_CTXMASK] = 0.0 if hf == 1 else NEG
        ppc_[:, PP_HASPREV] = 1.0 if hf == 1 else 0.0
        m["pp"] = ppc_
        maps.append(m)
    return maps


def run(inp, layers, do_mixer=True, do_ffn=True):
    nc = build(layers, do_mixer, do_ffn)
    maps = make_in_maps(inp, layers, do_mixer, do_ffn)
    res = run_bass_kernel_spmd(nc, maps, core_ids=list(range(NCORES)))
    out = np.empty((4, 2 * T, D), np.float32)
    for c in range(NCORES):
        b, hf = c // 2, c % 2
        out[b, hf * T:(hf + 1) * T, :] = np.asarray(res.results[c]["outT"]).T
    return out


def kernel(**inputs):
    return run(inputs, [0, 1, 2, 3])
```

```python
import contextlib
import math
import numpy as np
import concourse.bass as bass
import concourse.mybir as mybir
from concourse.bass_utils import run_bass_kernel_spmd

F32 = mybir.dt.float32
BF16 = mybir.dt.bfloat16
AF = mybir.ActivationFunctionType
ALU = mybir.AluOpType
AX = mybir.AxisListType

D = 2048
T = 1024
FC = 16
H = 16
F = 5632
NG = F // 512
DEPTH = 4
ALPHA = (2.0 * DEPTH) ** 0.25
LN_EPS = 1e-5
QSCALE = 128 ** -0.5
NEG = -30000.0
NBLK = 14
NCORES = 8
GROUPS = [[0, 1], [2, 3], [4, 5], [6, 7]]
DEBUG = {}

PP_LN = 0
PP_BF = 256
PP_CTXMASK = 258
PP_HASPREV = 259
NPP = 260
CP_BPW1 = 0
CP_WDW = 32
CP_BDW = 528
CP_LNG = 544
CP_LNB = 560
CP_BPW2 = 576
NCP = 592
CS_ONES = 0
CS_TRI = 128
CS_ID = 256
CS_SEL = 272
CS_IDB = 272 + 2048
NCS = 272 + 2048 + 128


class Buf:
    __slots__ = ("name", "w", "rs")

    def __init__(self, name):
        self.name = name
        self.w = None
        self.rs = {}


class Op:
    __slots__ = ("waits", "fn", "inc")

    def __init__(self, waits, fn, inc):
        self.waits, self.fn, self.inc = waits, fn, inc


ENG = ["pe", "act", "dve", "pool", "sp"]


class Sched:
    def __init__(self, nc, stack):
        self.nc, self.stack = nc, stack
        self.ops = {e: [] for e in ENG}
        self.sems = {}
        self.count = {}
        self.eng_key = {}
        self.eng_n = {e: 0 for e in ENG}
        self.seen = {e: {} for e in ENG}
        self.dry = False
        self.final = []

    def _sem(self, key):
        if key not in self.sems:
            self.sems[key] = self.stack.enter_context(self.nc.semaphore(key))
            self.count[key] = 0
        return key

    def _waits(self, eng, reads, writes, tok_sem, is_dma):
        need = {}

        def add(t, kind):
            if t is None:
                return
            sem, val, teng = t
            if (not is_dma) and teng == eng:
                if eng == "pe":
                    return
            if is_dma and sem == tok_sem and kind == "waw":
                return
            if need.get(sem, 0) < val:
                need[sem] = val

        for b in reads:
            add(b.w, "raw")
        for b in writes:
            add(b.w, "waw")
            for sem, (val, teng) in b.rs.items():
                add((sem, val, teng), "war")
        waits = []
        for sem, val in need.items():
            if self.seen[eng].get(sem, 0) < val:
                self.seen[eng][sem] = val
                waits.append((sem, val))
        return waits

    def _update(self, tok, reads, writes):
        for b in reads:
            b.rs[tok[0]] = (tok[1], tok[2])
        for b in writes:
            b.w = tok
            b.rs = {}

    def op(self, eng, fn, reads=(), writes=()):
        if self.dry:
            return
        k = self.eng_key.get(eng)
        if k is None or self.count[k] >= 3000:
            k = f"{eng}_{self.eng_n[eng]}"
            self.eng_n[eng] += 1
            self._sem(k)
            self.eng_key[eng] = k
        waits = self._waits(eng, reads, writes, k, False)
        self.count[k] += 1
        tok = (k, self.count[k], eng)
        self.ops[eng].append(Op(waits, fn, (k, 1)))
        self._update(tok, reads, writes)
        return tok

    def dma(self, queue, chan, fn, reads=(), writes=(), inc=16):
        if self.dry:
            return
        k = self._sem("d_" + chan)
        waits = self._waits(queue, reads, writes, k, True)
        self.count[k] += inc
        tok = (k, self.count[k], "dma")
        self.ops[queue].append(Op(waits, fn, (k, inc)))
        self._update(tok, reads, writes)
        return tok

    def check_deadlock(self):
        cnt = {k: 0 for k in self.sems}
        ptr = {e: 0 for e in ENG}
        progress = True
        while progress:
            progress = False
            for e in ENG:
                while ptr[e] < len(self.ops[e]):
                    o = self.ops[e][ptr[e]]
                    if all(cnt[s] >= v for s, v in o.waits):
                        cnt[o.inc[0]] += o.inc[1]
                        ptr[e] += 1
                        progress = True
                    else:
                        break
        stuck = {e: (ptr[e], len(self.ops[e])) for e in ENG if ptr[e] < len(self.ops[e])}
        if stuck:
            msg = []
            for e, (p, n) in stuck.items():
                o = self.ops[e][p]
                msg.append(f"{e} stuck at op {p}/{n} waits={[(s, v, cnt[s]) for s, v in o.waits if cnt[s] < v]}")
            raise RuntimeError("semaphore deadlock: " + "; ".join(msg))
        for s, v in self.final:
            assert cnt[s] >= v

    def emit(self):
        nc = self.nc
        self.check_deadlock()
        with nc.Block() as block:
            def mk(name):
                def f(e):
                    for o in self.ops[name]:
                        for sem, val in o.waits:
                            e.wait_ge(self.sems[sem], val)
                        last = o.fn(e)
                        last.then_inc(self.sems[o.inc[0]], o.inc[1])
                    if name == "sp":
                        for sem, val in self.final:
                            e.wait_ge(self.sems[sem], val)
                return f
            block.tensor(mk("pe"))
            block.scalar(mk("act"))
            block.vector(mk("dve"))
            block.gpsimd(mk("pool"))
            block.sync(mk("sp"))


class WStream:
    def __init__(self, S, ring_tensor, nr):
        self.S = S
        self.ring = ring_tensor
        self.nr = nr
        self.bufs = [Buf(f"ring{i}") for i in range(nr)]
        self.srcs = []
        self.next_get = 0
        self.next_load = 0
        self.after = ()

    def view(self, i, kind):
        r = self.ring[:, i * 8192:(i + 1) * 8192]
        if kind == "A":
            return r.rearrange("p (k n) -> p k n", k=16)
        return r.rearrange("p (k n) -> p k n", k=4)

    def _issue(self, idx):
        src, kind = self.srcs[idx]
        i = idx % self.nr
        dst = self.view(i, kind)

        def fn(e, dst=dst, src=src):
            return e.dma_start(out=dst, in_=src)
        self.S.dma("pool", f"ring{i}", fn, reads=(self.after if idx in (1, 2) else ()), writes=(self.bufs[i],))

    def get(self, src, kind, live=0):
        if self.S.dry:
            self.srcs.append((src, kind))
            return None, None
        idx = self.next_get
        self.next_get += 1
        while self.next_load < min(len(self.srcs), idx + self.nr - live):
            self._issue(self.next_load)
            self.next_load += 1
        i = idx % self.nr
        return self.view(i, kind), self.bufs[i]


def wslabA(w, c0):
    return w.rearrange("(k p) n -> p k n", p=128)[:, :, c0:c0 + 512]


def wslabB(w, r0):
    return w[r0:r0 + 512, :].rearrange("(k p) n -> p k n", p=128)


def build(layers, do_mixer=True, do_ffn=True, final_ln_only=False):
    nc = bass.Bass("TRN2", target_bir_lowering=False)
    dram = {}

    def din(name, shape, dt=F32):
        dram[name] = nc.dram_tensor(name, list(shape), dt, kind="ExternalInput").ap()
        return dram[name]

    xT = din("xT", [D, T])
    ppd = din("pp", [128, NPP])
    cstd = din("cst", [128, NCS])
    outT = nc.dram_tensor("outT", [D, T], F32, kind="ExternalOutput").ap()
    W = {}
    for i in layers:
        kind = i % 3
        if do_mixer:
            if kind in (0, 1):
                W[i, "qkv"] = din(f"wqkv{i}", [D, 3 * D])
                W[i, "o"] = din(f"wo{i}", [D, D])
                if kind == 0:
                    W[i, "wf"] = din(f"wf{i}", [128, 16 * 48])
                else:
                    W[i, "relb"] = din(f"relb{i}", [H, 128, 640])
            else:
                W[i, "pw1"] = din(f"pw1{i}", [D, 2 * D])
                W[i, "pw2"] = din(f"pw2{i}", [D, D])
                W[i, "cp"] = din(f"cp{i}", [128, NCP])
        if do_ffn:
            W[i, "g"] = din(f"wg{i}", [D, F])
            W[i, "u"] = din(f"wu{i}", [D, F])
            W[i, "d"] = din(f"wd{i}", [F, D])

    def dscr(name, shape, dt):
        return nc.dram_tensor(name, list(shape), dt, kind="Internal").ap()

    kT_own = [dscr(f"kT_own{p}", [D // 2, T], BF16) for p in range(2)]
    kT_g = [dscr(f"kT_g{p}", [D, T], BF16) for p in range(2)]
    v_own = [dscr(f"v_own{p}", [T, D // 2], BF16) for p in range(2)]
    v_g = [dscr(f"v_g{p}", [2 * T, D // 2], BF16) for p in range(2)]
    c_own = dscr("c_own", [16, T], F32)
    c_g = dscr("c_g", [32, T], F32)
    halo_own = [dscr(f"halo_own{g}", [512, 32], BF16) for g in range(4)]
    halo_g = [dscr(f"halo_g{g}", [1024, 32], BF16) for g in range(4)]
    yscr = dscr("yscr", [D, T], F32)

    stack = contextlib.ExitStack()
    with stack:
        def sb(name, shape, dt):
            return stack.enter_context(nc.sbuf_tensor("s_" + name, list(shape), dt))
        xr = sb("xr", [128, FC, T], F32)
        xb = sb("xb", [128, FC, T], BF16)
        ring = sb("ring", [128, 3 * 8192], BF16)
        arena = sb("arena", [128, NBLK * 1024], F32)
        pp = sb("pp", [128, NPP], F32)
        ones_f = sb("ones_f", [128, 128], F32)
        ones_b = sb("ones_b", [128, 128], BF16)
        tri_b = sb("tri_b", [128, 128], BF16)
        ident_b = sb("ident_b", [128, 128], BF16)
        ident = sb("ident", [16, 16], F32)
        wf_t = sb("wf_t", [128, 16 * 48], BF16)
        kb = sb("kb", [128, 256], F32)
        rl = sb("rl", [128, 512], F32)
        nbf = sb("nbf", [48, 2], F32)
        ps = stack.enter_context(nc.psum_tensor("psum_all", [128, 8, 512], F32))

        S = Sched(nc, stack)
        ws = WStream(S, ring, 3)
        B_xr = [Buf(f"xr{i}") for i in range(FC)]
        B_xb = [Buf(f"xb{i}") for i in range(FC)]
        B_ps = [Buf(f"ps{i}") for i in range(8)]
        B_ar = [Buf(f"ar{i}") for i in range(NBLK)]
        B_pp, B_cst, B_kb, B_rl, B_wf, B_nbf = Buf("pp"), Buf("cst"), Buf("kb"), Buf("rl"), Buf("wf"), Buf("nbf")
        B_co, B_cg = (Buf(n) for n in ("co", "cg"))
        B_y = [Buf("yscr0"), Buf("yscr1")]
        B_kTg, B_vg = ([Buf(n + str(p)) for p in range(2)] for n in ("kTg", "vg"))
        B_kTo, B_vo = ([[Buf(f"{n}{p}_{j}") for j in range(2)] for p in range(2)] for n in ("kTo", "vo"))
        B_ho = [Buf(f"ho{g}") for g in range(4)]
        B_hg = [Buf(f"hg{g}") for g in range(4)]
        B_out = Buf("out")
        ws.after = (B_xr[FC - 1],)
        B_pT = [Buf(f"pT{i}") for i in range(4)]
        B_kt = [[Buf(f"kt{s}c"), Buf(f"kt{s}o")] for s in range(2)]
        B_vt = [[Buf(f"vt{s}c"), Buf(f"vt{s}o")] for s in range(2)]
        B_lacc = [Buf("lacc0"), Buf("lacc1")]
        B_stg = [Buf("stg0"), Buf("stg1")]
        B_ps5 = [Buf("ps5a"), Buf("ps5b")]
        B_ps7 = [Buf("ps7a"), Buf("ps7b")]
        B_rl2 = [Buf("rl2a"), Buf("rl2b")]

        def alias_sync(subs, blk):
            def absorb(dst, srcb):
                for k_, v_ in srcb.rs.items():
                    if dst.rs.get(k_, (0, None))[0] < v_[0]:
                        dst.rs[k_] = v_
                if srcb.w is not None:
                    k_, val_, eng_ = srcb.w
                    if dst.rs.get(k_, (0, None))[0] < val_:
                        dst.rs[k_] = (val_, eng_)
            for s_ in subs:
                absorb(blk, s_)
            for s_ in subs:
                absorb(s_, blk)

        def alias_sync_all():
            alias_sync(B_stg, B_ar[8])
            alias_sync(B_pT, B_ar[6])
            alias_sync(B_rl2, B_rl)
            for s_ in range(2):
                alias_sync(B_kt[s_], B_ar[2 + s_])
                alias_sync(B_vt[s_], B_ar[4 + s_])

        def AF32(b0, nb=1):
            return arena[:, b0 * 1024:(b0 + nb) * 1024]

        def ABF(b0, nb=1):
            return arena[:, b0 * 1024:(b0 + nb) * 1024].bitcast(BF16)

        def AB(b0, nb=1):
            return B_ar[b0:b0 + nb]

        def ppc(c, n=1, rows=128):
            return pp[0:rows, c:c + n]

        state = {"pair": 0, "first_acc": True}
        pairs_all = [(0, 1), (2, 3), (4, 5), (6, 7)]
        state["pairs"] = pairs_all

        def next_pair():
            p = state["pairs"][state["pair"] % len(state["pairs"])]
            state["pair"] += 1
            return p

        def ps2(pair):
            return ps[:, pair[0]:pair[0] + 2, :]

        def v2(ap):
            return ap.rearrange("p (a b) -> p a b", a=2)

        def proj_fm(slab, sbuf_, evac, nchs=4):
            if state.get("after_ln") and len(state["pairs"]) == 4:
                state["after_ln"] = False
                banks = pairs_all
                for kc in range(16):
                    def fk(pe, kc=kc):
                        last = None
                        for nch in range(4):
                            for tb in range(2):
                                last = pe.matmul(ps[:, banks[nch][tb], :], lhsT=slab[:, kc, nch * 128:(nch + 1) * 128],
                                                 rhs=xb[:, kc, tb * 512:(tb + 1) * 512], start=(kc == 0), stop=(kc == 15))
                        return last
                    S.op("pe", fk, reads=[sbuf_, B_xb[kc]], writes=B_ps)
                for nch in range(4):
                    evac(nch, banks[nch])
                return
            state["after_ln"] = False
            for nch in range(nchs):
                pair = next_pair()

                def fn(pe, nch=nch, pair=pair):
                    last = None
                    for kc in range(16):
                        for tb in range(2):
                            last = pe.matmul(ps[:, pair[tb], :], lhsT=slab[:, kc, nch * 128:(nch + 1) * 128],
                                             rhs=xb[:, kc, tb * 512:(tb + 1) * 512], start=(kc == 0), stop=(kc == 15))
                    return last
                S.op("pe", fn, reads=[sbuf_] + B_xb, writes=[B_ps[pair[0]], B_ps[pair[1]]])
                evac(nch, pair)

        def proj_acc(src, src_bufs, nk, slab, sbuf_, last=False, bias=None):
            first = state["first_acc"]
            state["first_acc"] = False
            s1, s2, sqb = AF32(12), AF32(11), AF32(10)
            for n in range(16):
                pair = next_pair()

                def fn(pe, n=n, pair=pair):
                    last_i = None
                    for kc in range(nk):
                        for tb in range(2):
                            last_i = pe.matmul(ps[:, pair[tb], :], lhsT=slab[:, kc, n * 128:(n + 1) * 128],
                                               rhs=src[:, kc, tb * 512:(tb + 1) * 512], start=(kc == 0), stop=(kc == nk - 1))
                    return last_i
                S.op("pe", fn, reads=[sbuf_] + list(src_bufs), writes=[B_ps[pair[0]], B_ps[pair[1]]])
                rd = [B_ps[pair[0]], B_ps[pair[1]], B_xr[n]]
                if first:
                    assert bias is None
                    def fa(e, n=n, pair=pair):
                        return e.scalar_tensor_tensor(out=v2(xr[:, n, :]), in0=v2(xr[:, n, :]), scalar=ALPHA,
                                                      in1=ps2(pair), op0=ALU.mult, op1=ALU.add)
                elif bias is not None:
                    bc, bb = bias
                    rd = rd + list(bb)
                    def fa(e, n=n, pair=pair, bc=bc):
                        return e.scalar_tensor_tensor(out=v2(xr[:, n, :]), in0=ps2(pair), scalar=bc(n),
                                                      in1=v2(xr[:, n, :]), op0=ALU.add, op1=ALU.add)
                else:
                    def fa(e, n=n, pair=pair):
                        return e.tensor_tensor(out=v2(xr[:, n, :]), in0=ps2(pair), in1=v2(xr[:, n, :]), op=ALU.add)
                S.op("dve", fa, reads=rd, writes=[B_xr[n]])
                if last:
                    if n == 0:
                        S.op("pool", lambda e, n=n: e.tensor_copy(out=s1, in_=xr[:, n, :]), reads=[B_xr[n]], writes=AB(12))
                    else:
                        S.op("pool", lambda e, n=n: e.tensor_tensor(out=s1, in0=s1, in1=xr[:, n, :], op=ALU.add),
                             reads=[B_xr[n]] + AB(12), writes=AB(12))
                    S.op("act", lambda e, n=n: e.activation(out=sqb, in_=xr[:, n, :], func=AF.Square), reads=[B_xr[n]], writes=AB(10))
                    if n == 0:
                        S.op("dve", lambda e: e.tensor_copy(out=s2, in_=sqb), reads=AB(10), writes=AB(11))
                    else:
                        S.op("dve", lambda e: e.tensor_tensor(out=s2, in0=s2, in1=sqb, op=ALU.add), reads=AB(10) + AB(11), writes=AB(11))

        def ln_stats_finish(s1, s2, b_s1, b_s2, mean, rstd, b_mean, b_rstd, tmp, b_tmp):
            pa, pb = next_pair(), next_pair()

            def f1(pe):
                last = None
                for tb in range(2):
                    last = pe.matmul(ps[:, pa[tb], :], lhsT=ones_f[:, :], rhs=s1[:, tb * 512:(tb + 1) * 512], start=True, stop=True)
                for tb in range(2):
                    last = pe.matmul(ps[:, pb[tb], :], lhsT=ones_f[:, :], rhs=s2[:, tb * 512:(tb + 1) * 512], start=True, stop=True)
                return last
            S.op("pe", f1, reads=[b_s1, b_s2, B_cst], writes=[B_ps[pa[0]], B_ps[pa[1]], B_ps[pb[0]], B_ps[pb[1]]])
            S.op("act", lambda e: e.activation(out=v2(mean), in_=ps2(pa), func=AF.Copy, scale=1.0 / D),
                 reads=[B_ps[pa[0]], B_ps[pa[1]]], writes=[b_mean])
            S.op("dve", lambda e: e.tensor_tensor(out=tmp, in0=mean, in1=mean, op=ALU.mult), reads=[b_mean], writes=[b_tmp])
            S.op("dve", lambda e: e.scalar_tensor_tensor(out=v2(rstd), in0=ps2(pb), scalar=1.0 / D, in1=v2(tmp),
                                                         op0=ALU.mult, op1=ALU.subtract),
                 reads=[B_ps[pb[0]], B_ps[pb[1]], b_tmp], writes=[b_rstd])
            S.op("act", lambda e: e.activation(out=rstd, in_=rstd, func=AF.Sqrt, bias=epsc[:, 0:1], scale=1.0),
                 reads=[b_rstd, B_cst], writes=[b_rstd])
            S.op("dve", lambda e: e.reciprocal(out=rstd, in_=rstd), reads=[b_rstd], writes=[b_rstd])

        def layer_norm(gcol, bcol, final=False):
            alias_sync_all()
            s1, s2, mean, rstd = AF32(12), AF32(11), AF32(0), AF32(1)
            t = [AF32(2), AF32(3)]
            yst = [AF32(4), AF32(5)]
            ln_stats_finish(s1, s2, B_ar[12], B_ar[11], mean, rstd, B_ar[0], B_ar[1], AF32(6), B_ar[6])
            t = [AF32(2), AF32(3), AF32(7), AF32(8)]
            tbuf = [AB(2), AB(3), AB(7), AB(8)]

            def ln_sub(fc):
                tt, bt = t[fc % 4], tbuf[fc % 4]
                S.op("dve", lambda e: e.tensor_tensor(out=tt, in0=xr[:, fc, :], in1=mean, op=ALU.subtract),
                     reads=[B_xr[fc], B_ar[0]], writes=bt)

            def ln_rest(fc):
                tt, bt = t[fc % 4], tbuf[fc % 4]
                S.op("dve", lambda e: e.tensor_tensor(out=tt, in0=tt, in1=rstd, op=ALU.mult), reads=bt + [B_ar[1]], writes=bt)
                if not final:
                    S.op("act", lambda e: e.activation(out=xb[:, fc, :], in_=tt, func=AF.Identity,
                                                       scale=ppc(gcol + fc), bias=ppc(bcol + fc)),
                         reads=bt + [B_pp], writes=[B_xb[fc]])
                    S.op("act", lambda e: e.activation(out=xr[:, fc, :], in_=tt, func=AF.Identity,
                                                       scale=ppc(gcol + fc), bias=ppc(bcol + fc)),
                         reads=bt + [B_pp], writes=[B_xr[fc]])
                else:
                    ys = yst[fc % 2]
                    by = AB(4 + fc % 2)
                    S.op("act", lambda e: e.activation(out=ys, in_=tt, func=AF.Identity,
                                                       scale=ppc(gcol + fc), bias=ppc(bcol + fc)),
                         reads=bt + [B_pp], writes=by)
                    tok = S.dma("sp", f"out{fc % 2}", lambda e: e.dma_start(out=outT[fc * 128:(fc + 1) * 128, :], in_=ys),
                                reads=by, writes=[B_out])
                    if tok is not None:
                        S.final.append((tok[0], tok[1]))
            for f2 in range(0, FC, 2):
                ln_sub(f2)
                ln_sub(f2 + 1)
                ln_rest(f2)
                ln_rest(f2 + 1)
            alias_sync_all()
            state["first_acc"] = True
            state["after_ln"] = True

        def ffn(i):
            hg = ABF(0, 2).rearrange("p (k t) -> p k t", k=4)
            hb = [ABF(2, 2).rearrange("p (k t) -> p k t", k=4), ABF(4, 2).rearrange("p (k t) -> p k t", k=4)]
            hbb = [AB(2, 2), AB(4, 2)]
            for g in range(NG):
                slab, sbuf_ = ws.get(wslabA(W[i, "g"], g * 512), "A")

                def ev_g(nch, pair):
                    S.op("act", lambda e: e.activation(out=v2(hg[:, nch, :]), in_=ps2(pair), func=AF.Silu),
                         reads=[B_ps[pair[0]], B_ps[pair[1]]], writes=AB(0, 2))
                if not S.dry:
                    proj_fm(slab, sbuf_, ev_g)
                slab, sbuf_ = ws.get(wslabA(W[i, "u"], g * 512), "A")
                hcur, hcb = hb[g % 2], hbb[g % 2]

                def ev_u(nch, pair, hcur=hcur, hcb=hcb):
                    S.op("dve", lambda e: e.tensor_tensor(out=v2(hcur[:, nch, :]), in0=ps2(pair), in1=v2(hg[:, nch, :]), op=ALU.mult),
                         reads=[B_ps[pair[0]], B_ps[pair[1]]] + AB(0, 2), writes=hcb)
                if not S.dry:
                    proj_fm(slab, sbuf_, ev_u)
                slab, sbuf_ = ws.get(wslabB(W[i, "d"], g * 512), "B")
                if not S.dry:
                    proj_acc(hcur, hcb, 4, slab, sbuf_, last=(g == NG - 1))

        def attn_phase_a(i, fox):
            wq = W[i, "qkv"]
            stg = [ABF(8)[:, 0:1024], ABF(8)[:, 1024:2048]]
            k = [0]

            def gather(name, p_, own, gat, b_own, b_gat):
                S.dma("pool", f"cc_{name}{p_}", lambda e: e.collective_compute("AllGather", ALU.bypass, replica_groups=GROUPS,
                                                                               ins=[own[:, :]], outs=[gat[:, :]]),
                      reads=b_own, writes=[b_gat], inc=1)

            for s in range(4):
                slab, sbuf_ = ws.get(wslabA(wq, D + s * 512), "A")

                def ev_k(nch, pair, s=s):
                    j = k[0] % 2
                    k[0] += 1
                    S.op("act", lambda e: e.activation(out=v2(stg[j]), in_=ps2(pair), func=AF.Copy),
                         reads=[B_ps[pair[0]], B_ps[pair[1]]], writes=[B_stg[j]])
                    r0 = (s * 4 + nch) * 128
                    hp, rr = r0 // 1024, r0 % 1024
                    S.dma("sp", f"stg{j}", lambda e: e.dma_start(out=kT_own[hp][rr:rr + 128, :], in_=stg[j]),
                          reads=[B_stg[j]], writes=[B_kTo[hp][j]])
                if not S.dry:
                    proj_fm(slab, sbuf_, ev_k)
                    if s % 2 == 1:
                        gather("k", s // 2, kT_own[s // 2], kT_g[s // 2], B_kTo[s // 2], B_kTg[s // 2])
            if fox and not S.dry:
                fox_forget(i)

            def v_tile(s, tt, slab, sbuf_):
                pair = next_pair()
                bank = pair[0]

                def fn(pe):
                    last = None
                    for kc in range(16):
                        last = pe.matmul(ps[:, bank, :], lhsT=xb[:, kc, tt * 128:(tt + 1) * 128], rhs=slab[:, kc, :],
                                         start=(kc == 0), stop=(kc == 15))
                    return last
                S.op("pe", fn, reads=[sbuf_] + B_xb, writes=[B_ps[bank]])
                j = k[0] % 2
                k[0] += 1
                if tt % 2 == 0:
                    S.op("act", lambda e: e.activation(out=stg[j][:, 0:512], in_=ps[:, bank, :], func=AF.Copy),
                         reads=[B_ps[bank]], writes=[B_stg[j]])
                else:
                    S.op("dve", lambda e: e.tensor_copy(out=stg[j][:, 0:512], in_=ps[:, bank, :]),
                         reads=[B_ps[bank]], writes=[B_stg[j]])
                hp, c0 = s // 2, (s % 2) * 512
                S.dma("sp", f"stg{j}", lambda e: e.dma_start(out=v_own[hp][tt * 128:(tt + 1) * 128, c0:c0 + 512], in_=stg[j][:, 0:512]),
                      reads=[B_stg[j]], writes=[B_vo[hp][j]])

            for s in range(4):
                slab, sbuf_ = ws.get(wslabA(wq, 2 * D + s * 512), "A")
                if S.dry:
                    continue
                for tt in range(8):
                    v_tile(s, tt, slab, sbuf_)
                if s % 2 == 1:
                    gather("v", s // 2, v_own[s // 2], v_g[s // 2], B_vo[s // 2], B_vg[s // 2])
            if fox and not S.dry:
                fox_ctx_bias()
            alias_sync(B_stg, B_ar[8])

        def fox_forget(i):
            j = i // 3
            cn, tmpe, ones48 = AF32(12), AF32(13), AF32(9)
            wfv = wf_t[:, :].rearrange("p (k m) -> p k m", k=16)
            S.dma("pool", "wf", lambda e: e.dma_start(out=wf_t[:, :], in_=W[i, "wf"]), reads=(), writes=[B_wf])
            pair = next_pair()

            def fz(pe):
                last = None
                for kc in range(16):
                    for tb in range(2):
                        last = pe.matmul(ps[0:48, pair[tb], :], lhsT=wfv[:, kc, :], rhs=xb[:, kc, tb * 512:(tb + 1) * 512],
                                         start=(kc == 0), stop=(kc == 15))
                return last
            S.op("pe", fz, reads=[B_wf] + B_xb, writes=[B_ps[pair[0]], B_ps[pair[1]]])
            S.op("act", lambda e: e.activation(out=v2(tmpe[0:48, :]), in_=ps[0:48, pair[0]:pair[0] + 2, :], func=AF.Exp,
                                               bias=nbf[0:48, j:j + 1], scale=-1.0),
                 reads=[B_ps[pair[0]], B_ps[pair[1]], B_nbf], writes=AB(13))
            S.op("act", lambda e: e.activation(out=tmpe[0:48, :], in_=tmpe[0:48, :], func=AF.Ln, bias=onec[0:48, 0:1], scale=1.0),
                 reads=AB(13) + [B_cst], writes=AB(13))
            S.op("pool", lambda e: e.memset(ones48[0:48, :], 1.0), reads=(), writes=AB(9))
            S.op("dve", lambda e: e.tensor_tensor_scan(out=cn[0:48, :], data0=ones48[0:48, :], data1=tmpe[0:48, :], initial=0.0,
                                                        op0=ALU.mult, op1=ALU.add),
                 reads=AB(9) + AB(13), writes=AB(12))
            S.dma("sp", "cown", lambda e: e.dma_start(out=c_own[:, :], in_=cn[0:16, :]), reads=AB(12), writes=[B_co])
            S.dma("pool", "cc_c", lambda e: e.collective_compute("AllGather", ALU.bypass, replica_groups=GROUPS,
                                                               ins=[c_own[:, :]], outs=[c_g[:, :]]),
                  reads=[B_co], writes=[B_cg], inc=1)
            hl = ABF(11)[:, 0:1024]
            hi32 = ABF(11)[:, 1024:2048]
            S.op("act", lambda e: e.activation(out=hl[0:48, :], in_=cn[0:48, :], func=AF.Copy, scale=-1.0),
                 reads=AB(12), writes=AB(11))
            S.op("act", lambda e: e.activation(out=hi32[32:48, :], in_=cn[32:48, :], func=AF.Copy, scale=-1.0),
                 reads=AB(12), writes=AB(11))
            S.op("dve", lambda e: e.scalar_tensor_tensor(out=hl[32:48, :], in0=cn[32:48, :], scalar=-1.0, in1=hi32[32:48, :],
                                                         op0=ALU.mult, op1=ALU.subtract),
                 reads=AB(12) + AB(11), writes=AB(11))
            pr = next_pair()

            def ft(pe):
                last = None
                for t_ in range(8):
                    last = pe.matmul(ps[:, pr[0], t_ * 16:(t_ + 1) * 16], lhsT=cn[0:16, t_ * 128:(t_ + 1) * 128], rhs=ident[:, :],
                                     start=True, stop=True)
                return last
            S.op("pe", ft, reads=AB(12) + [B_cst], writes=[B_ps[pr[0]]])
            S.op("dve", lambda e: e.tensor_copy(out=kb[:, 0:128], in_=ps[:, pr[0], 0:128]), reads=[B_ps[pr[0]]], writes=[B_kb])

        def fox_ctx_bias():
            cctx = AF32(13)
            S.dma("sp", "cctx", lambda e: e.dma_start(out=cctx[0:16, :], in_=c_g[0:16, :]), reads=[B_cg], writes=AB(13))
            S.op("dve", lambda e: e.tensor_scalar(out=cctx[0:16, :], in0=cctx[0:16, :], scalar1=cctx[0:16, 1023:1024], scalar2=None,
                                                  op0=ALU.subtract),
                 reads=AB(13), writes=AB(13))
            pr2 = next_pair()

            def ft2(pe):
                last = None
                for t_ in range(8):
                    last = pe.matmul(ps[:, pr2[0], t_ * 16:(t_ + 1) * 16], lhsT=cctx[0:16, t_ * 128:(t_ + 1) * 128], rhs=ident[:, :],
                                     start=True, stop=True)
                return last
            S.op("pe", ft2, reads=AB(13) + [B_cst], writes=[B_ps[pr2[0]]])
            S.op("dve", lambda e: e.tensor_scalar(out=kb[:, 128:256], in0=ps[:, pr2[0], 0:128], scalar1=ppc(PP_CTXMASK), scalar2=None,
                                                  op0=ALU.add),
                 reads=[B_ps[pr2[0]], B_pp], writes=[B_kb])

        def load_kv(h, slot, ctx_lo):
            kt = ABF(2 + slot)
            vt = ABF(4 + slot).rearrange("p (t d) -> p t d", t=16)
            hp, rr = h // 8, (h % 8) * 128
            t0 = ctx_lo // 128
            S.dma("sp", f"kt{slot}c", lambda e: e.dma_start(out=kt[:, ctx_lo:1024], in_=kT_g[hp][rr:rr + 128, ctx_lo:1024]),
                  reads=[B_kTg[hp]], writes=[B_kt[slot][0]])
            S.dma("sp", f"kt{slot}o", lambda e: e.dma_start(out=kt[:, 1024:2048], in_=kT_own[hp][rr:rr + 128, :]),
                  reads=B_kTo[hp], writes=[B_kt[slot][1]])
            S.dma("sp", f"vt{slot}c", lambda e: e.dma_start(
                out=vt[:, t0:8, :], in_=v_g[hp][ctx_lo:1024, rr:rr + 128].rearrange("(t p) d -> p t d", p=128)),
                reads=[B_vg[hp]], writes=[B_vt[slot][0]])
            S.dma("sp", f"vt{slot}o", lambda e: e.dma_start(
                out=vt[:, 8:16, :], in_=v_own[hp][:, rr:rr + 128].rearrange("(t p) d -> p t d", p=128)),
                reads=B_vo[hp], writes=[B_vt[slot][1]])
            return kt, vt

        def q_proj(i, hgp, qT):
            slab, sbuf_ = ws.get(wslabA(W[i, "qkv"], hgp * 512), "A")

            def ev_q(nch, pair):
                S.op("act", lambda e: e.activation(out=v2(qT[:, nch, :]), in_=ps2(pair), func=AF.Copy, scale=QSCALE),
                     reads=[B_ps[pair[0]], B_ps[pair[1]]], writes=AB(0, 2))
            if not S.dry:
                proj_fm(slab, sbuf_, ev_q)

        def fox_layer(i):
            attn_phase_a(i, True)
            qT = ABF(0, 2).rearrange("p (k t) -> p k t", k=4)
            ob = ABF(8, 2).rearrange("p (k t) -> p k t", k=4)
            pT = [ABF(6)[:, s * 512:(s + 1) * 512] for s in range(4)]
            selv = ABF(10).rearrange("p (h m) -> p h m", h=16)
            hl = ABF(11)[:, 0:1024]
            state["pairs"] = [(0, 1), (2, 3)]
            kvs = {}
            if not S.dry:
                S.dma("pool", "sel", lambda e: e.dma_start(out=ABF(10)[0:48, :], in_=cstd[0:48, CS_SEL:CS_SEL + 2048]),
                      reads=(), writes=AB(10))
                kvs[0] = load_kv(0, 0, 0)

            qbc = [0]

            def qblock(h, hh, kt, vt, slot, qb):
                bo, bl = (6, 7) if qbc[0] % 2 == 0 else (2, 3)
                qbc[0] += 1
                tiles = [(j, 0, False) for j in range(8)]
                for jo in range(4 * qb + 4):
                    off = max(0, (jo - 4 * qb) * 128)
                    tiles.append((8 + jo, off, jo >= 4 * qb))
                nt = len(tiles)
                q0 = qb * 512

                def qk(idx):
                    j, off, diag = tiles[idx]
                    bank = 4 + idx % 2

                    def fn(pe):
                        pe.matmul(ps[:, bank, off:512], lhsT=kt[:, j * 128:(j + 1) * 128], rhs=qT[:, hh, q0 + off:q0 + 512],
                                  start=True, stop=False, skip_group_check=True)
                        return pe.matmul(ps[:, bank, off:512], lhsT=selv[0:48, h, :], rhs=hl[0:48, q0 + off:q0 + 512],
                                         start=False, stop=True, skip_group_check=True)
                    S.op("pe", fn, reads=B_kt[slot] + AB(0, 2) + AB(10) + AB(11), writes=[B_ps[bank]])
                    col = (128 + j * 16 + h) if j < 8 else ((j - 8) * 16 + h)
                    pt = pT[idx % 4]
                    bpt = [B_pT[idx % 4]]
                    S.op("act", lambda e: e.activation(out=pt[:, off:512], in_=ps[:, bank, off:512], func=AF.Exp,
                                                       bias=kb[:, col:col + 1], scale=1.0),
                         reads=[B_ps[bank], B_kb], writes=bpt)
                    if diag:
                        S.op("dve", lambda e: e.tensor_tensor(out=pt[:, off:off + 128], in0=pt[:, off:off + 128], in1=tri_b[:, :],
                                                              op=ALU.mult),
                             reads=bpt + [B_cst], writes=bpt)

                def pv(idx):
                    j, off, diag = tiles[idx]
                    pt = pT[idx % 4]

                    def fn(pe):
                        pe.matmul(ps[:, bo, off:512], lhsT=vt[:, j, :], rhs=pt[:, off:512], start=(idx == 0), stop=(idx == nt - 1),
                                  skip_group_check=True)
                        return pe.matmul(ps[:, bl, off:512], lhsT=ones_b[:, :], rhs=pt[:, off:512], start=(idx == 0),
                                         stop=(idx == nt - 1), skip_group_check=True)
                    S.op("pe", fn, reads=B_vt[slot] + [B_pT[idx % 4], B_cst], writes=[B_ps[bo], B_ps[bl]])
                qk(0)
                for idx in range(nt):
                    if idx + 1 < nt:
                        qk(idx + 1)
                    pv(idx)
                S.op("dve", lambda e: e.reciprocal(out=rl[:, :], in_=ps[:, bl, :]), reads=[B_ps[bl]], writes=[B_rl])
                S.op("dve", lambda e: e.tensor_tensor(out=ob[:, hh, q0:q0 + 512], in0=ps[:, bo, :], in1=rl[:, :], op=ALU.mult),
                     reads=[B_ps[bo], B_rl], writes=AB(8, 2))

            for hgp in range(4):
                q_proj(i, hgp, qT)
                for hh in range(4):
                    if S.dry:
                        continue
                    h = hgp * 4 + hh
                    kt, vt = kvs[h]
                    if h + 1 < 16:
                        kvs[h + 1] = load_kv(h + 1, (h + 1) % 2, 0)
                    for qb in range(2):
                        qblock(h, hh, kt, vt, h % 2, qb)
                slab, sbuf_ = ws.get(wslabB(W[i, "o"], hgp * 512), "B")
                if not S.dry:
                    proj_acc(ob, AB(8, 2), 4, slab, sbuf_, last=(hgp == 3))
            state["pairs"] = pairs_all

        def rel_layer(i):
            attn_phase_a(i, False)
            qT = ABF(0, 2).rearrange("p (k t) -> p k t", k=4)
            ob = ABF(8, 2).rearrange("p (k t) -> p k t", k=4)
            pT5 = [ABF(6)[:, 0:640], ABF(6)[:, 1024:1664]]
            sbt = [AF32(7)[:, 0:640], AF32(11)[:, 0:640]]
            sbb = [AB(7), AB(11)]
            relt = [AF32(12)[:, 0:640], AF32(13)[:, 0:640]]
            state["pairs"] = [(0, 1), (2, 3)]
            sc = [(4, 5), (6, 7)]
            kvs = {}
            if not S.dry:
                kvs[0] = load_kv(0, 0, 512)
                S.dma("sp", "relb0", lambda e: e.dma_start(out=relt[0], in_=W[i, "relb"][0, :, :]), reads=(), writes=AB(12))
            cnt = [0]

            def qtile(h, hh, kt, vt, slot, qi):
                c = cnt[0] % 2
                cnt[0] += 1
                bA, bB = sc[c]
                bO = 2 * c
                cB = 0
                rt = relt[slot]
                rtb = AB(12 + slot)
                nctx = max(0, 4 - qi)
                ktiles = [8 + qi - 4 + r for r in range(5)]
                sbc, sbcb = sbt[c], sbb[c]
                pt = pT5[c]
                bpt = [B_pT[c]]
                oc = 0

                def qk_part():
                    def fqk(pe):
                        last = None
                        for r in range(5):
                            dst = ps[:, bA, r * 128:(r + 1) * 128] if r < 4 else ps[:, bB, cB:cB + 128]
                            j = ktiles[r]
                            last = pe.matmul(dst, lhsT=kt[:, j * 128:(j + 1) * 128], rhs=qT[:, hh, qi * 128:(qi + 1) * 128],
                                             start=True, stop=True, skip_group_check=True)
                        return last
                    S.op("pe", fqk, reads=B_kt[slot] + AB(0, 2), writes=[B_ps[bA], B_ps[bB]])
                    flat = ps[:, bA:bA + 2, :].rearrange("p a b -> p (a b)")
                    S.op("dve", lambda e: e.tensor_tensor(out=sbc[:, 0:640], in0=flat[:, 0:640], in1=rt[:, 0:640], op=ALU.add),
                         reads=[B_ps[bA], B_ps[bB]] + rtb, writes=sbcb)
                    nc_ = nctx * 128
                    if nctx > 0:
                        S.op("act", lambda e: e.activation(out=pt[:, 0:nc_], in_=sbc[:, 0:nc_], func=AF.Exp, bias=ppc(PP_CTXMASK), scale=1.0),
                             reads=sbcb + [B_pp], writes=bpt)
                    S.op("act", lambda e: e.activation(out=pt[:, nc_:640], in_=sbc[:, nc_:640], func=AF.Exp), reads=sbcb, writes=bpt)

                def pv_part():
                    def fpv(pe):
                        last = None
                        for r in range(5):
                            j = ktiles[r]
                            pe.matmul(ps[:, bO, oc:oc + 128], lhsT=vt[:, j, :], rhs=pt[:, r * 128:(r + 1) * 128],
                                      start=(r == 0), stop=(r == 4), skip_group_check=True)
                        for r in range(5):
                            last = pe.matmul(ps[:, bO, oc + 128:oc + 256], lhsT=ones_b[:, :], rhs=pt[:, r * 128:(r + 1) * 128],
                                             start=(r == 0), stop=(r == 4), skip_group_check=True)
                        return last
                    S.op("pe", fpv, reads=B_vt[slot] + bpt + [B_cst], writes=[B_ps[bO]])
                    rc = c * 128
                    S.op("dve", lambda e: e.reciprocal(out=rl[:, rc:rc + 128], in_=ps[:, bO, oc + 128:oc + 256]),
                         reads=[B_ps[bO]], writes=[B_rl2[c]])
                    S.op("dve", lambda e: e.tensor_tensor(out=ob[:, hh, qi * 128:(qi + 1) * 128], in0=ps[:, bO, oc:oc + 128],
                                                          in1=rl[:, rc:rc + 128], op=ALU.mult),
                         reads=[B_ps[bO], B_rl2[c]], writes=AB(8, 2))
                return qk_part, pv_part

            def relb_load(hn):
                S.dma("sp", f"relb{hn % 2}", lambda e: e.dma_start(out=relt[hn % 2], in_=W[i, "relb"][hn, :, :]),
                      reads=(), writes=AB(12 + hn % 2))

            if not S.dry:
                kvs[1] = load_kv(1, 1, 512)
                relb_load(1)
            for hgp in range(4):
                q_proj(i, hgp, qT)
                if not S.dry:
                    steps = []
                    for hh in range(4):
                        h = hgp * 4 + hh
                        kt = ABF(2 + h % 2)
                        vt = ABF(4 + h % 2).rearrange("p (t d) -> p t d", t=16)
                        for qi in range(8):
                            steps.append((h, qi) + qtile(h, hh, kt, vt, h % 2, qi))
                    skew = not DEBUG.get("noskew")
                    if skew:
                        steps[0][2]()
                    for t_, (h, qi, qk_, pv_) in enumerate(steps):
                        if not skew:
                            qk_()
                        elif t_ + 1 < len(steps):
                            steps[t_ + 1][2]()
                        pv_()
                        if qi == 7 and h + 2 < 16:
                            kvs[h + 2] = load_kv(h + 2, h % 2, 512)
                            relb_load(h + 2)
                slab, sbuf_ = ws.get(wslabB(W[i, "o"], hgp * 512), "B")
                if not S.dry:
                    proj_acc(ob, AB(8, 2), 4, slab, sbuf_, last=(hgp == 3))
            state["pairs"] = pairs_all

        def conv_layer(i):
            cpt = AF32(13)[:, 0:NCP]
            ub = ABF(0, 3)[:, 0:4 * 1056].rearrange("p (k t) -> p k t", k=4)
            sg = [AF32(3), AF32(4)]
            dg = [ABF(5, 2)[:, 0:31 * 128].rearrange("p (k m) -> p k m", k=31), ABF(7, 2)[:, 0:31 * 128].rearrange("p (k m) -> p k m", k=31)]
            dgb = [AB(5, 2), AB(7, 2)]
            acc, sq, s1c, s2c = AF32(9), AF32(10), AF32(11), AF32(12)
            hal = rl[:, 0:64].bitcast(BF16).rearrange("p (k t) -> p k t", k=4)
            if not S.dry:
                S.dma("sp", "cp", lambda e: e.dma_start(out=cpt, in_=W[i, "cp"]), reads=(), writes=AB(13))

            gk = [0]

            def glu_unit(g, nch, tb, slab_g, sb_g, slab_a, sb_a):
                state["after_ln"] = False
                fcg = g * 4 + nch
                pair = next_pair()
                j = gk[0] % 2
                gk[0] += 1

                def fn(pe):
                    last = None
                    for which, slab in ((0, slab_g), (1, slab_a)):
                        for kc in range(16):
                            last = pe.matmul(ps[:, pair[which], :], lhsT=slab[:, kc, nch * 128:(nch + 1) * 128],
                                             rhs=xb[:, kc, tb * 512:(tb + 1) * 512], start=(kc == 0), stop=(kc == 15))
                    return last
                S.op("pe", fn, reads=[sb_g, sb_a] + B_xb, writes=[B_ps[pair[0]], B_ps[pair[1]]])
                S.op("act", lambda e: e.activation(out=sg[j][:, 0:512], in_=ps[:, pair[0], :], func=AF.Sigmoid,
                                                   bias=cpt[:, CP_BPW1 + 16 + fcg:CP_BPW1 + 17 + fcg], scale=1.0),
                     reads=[B_ps[pair[0]]] + AB(13), writes=AB(3 + j))
                c0 = 32 + tb * 512
                S.op("dve", lambda e: e.scalar_tensor_tensor(
                    out=ub[:, nch, c0:c0 + 512], in0=ps[:, pair[1], :], scalar=cpt[:, CP_BPW1 + fcg:CP_BPW1 + fcg + 1],
                    in1=sg[j][:, 0:512], op0=ALU.add, op1=ALU.mult),
                    reads=[B_ps[pair[1]]] + AB(13) + AB(3 + j), writes=AB(0, 3))

            def halo_send(g):
                S.dma("sp", "halo", lambda e: e.dma_start(out=halo_own[g].rearrange("(k p) t -> p k t", p=128), in_=ub[:, :, 1024:1056]),
                      reads=AB(0, 3), writes=[B_ho[g]])
                S.dma("pool", f"cc_h{g}", lambda e: e.collective_compute("AllGather", ALU.bypass, replica_groups=GROUPS,
                                                                         ins=[halo_own[g][:, :]], outs=[halo_g[g][:, :]]),
                      reads=[B_ho[g]], writes=[B_hg[g]], inc=1)

            def halo_recv(g):
                S.dma("sp", "halo", lambda e: e.dma_start(out=hal, in_=halo_g[g][0:512, :].rearrange("(k p) t -> p k t", p=128)),
                      reads=[B_hg[g]], writes=[B_rl])
                S.op("dve", lambda e: e.tensor_scalar(out=ub[:, :, 0:32], in0=hal, scalar1=ppc(PP_HASPREV), scalar2=None, op0=ALU.mult),
                     reads=[B_rl, B_pp], writes=AB(0, 3))

            conv_pairs = [(0, 1), (2, 3), (4, 5), (6, 7)]

            def diag_build(g, nch):
                fcg = g * 4 + nch
                wc = CP_WDW + fcg * 31
                d_, db_ = dg[nch % 2], dgb[nch % 2]

                def fdiag(e):
                    last = None
                    for k_ in range(31):
                        last = e.tensor_scalar(out=d_[:, k_, :], in0=ident_b[:, :], scalar1=cpt[:, wc + k_:wc + k_ + 1], scalar2=None,
                                               op0=ALU.mult)
                    return last
                S.op("dve", fdiag, reads=AB(13) + [B_cst], writes=db_)

            def conv_mm(g, nch, tb):
                d_, db_ = dg[nch % 2], dgb[nch % 2]
                bank = conv_pairs[nch][tb]

                def fconv(pe):
                    last = None
                    for k_ in range(31):
                        c0 = 2 + k_ + tb * 512
                        last = pe.matmul(ps[:, bank, :], lhsT=d_[:, k_, :], rhs=ub[:, nch, c0:c0 + 512],
                                         start=(k_ == 0), stop=(k_ == 30))
                    return last
                S.op("pe", fconv, reads=db_ + AB(0, 3), writes=[B_ps[bank]])

            def conv_evac(g, nch):
                fcg = g * 4 + nch
                pair = conv_pairs[nch]
                S.op("act", lambda e: e.activation(out=v2(acc), in_=ps2(pair), func=AF.Identity,
                                                   bias=cpt[:, CP_BDW + fcg:CP_BDW + fcg + 1], scale=1.0),
                     reads=[B_ps[pair[0]], B_ps[pair[1]]] + AB(13), writes=AB(9))
                first = (g == 0 and nch == 0)
                if first:
                    S.op("dve", lambda e: e.tensor_copy(out=s1c, in_=acc), reads=AB(9), writes=AB(11))
                else:
                    S.op("dve", lambda e: e.tensor_tensor(out=s1c, in0=s1c, in1=acc, op=ALU.add), reads=AB(9) + AB(11), writes=AB(11))
                S.op("act", lambda e: e.activation(out=sq, in_=acc, func=AF.Square), reads=AB(9), writes=AB(10))
                if first:
                    S.op("dve", lambda e: e.tensor_copy(out=s2c, in_=sq), reads=AB(10), writes=AB(12))
                else:
                    S.op("dve", lambda e: e.tensor_tensor(out=s2c, in0=s2c, in1=sq, op=ALU.add), reads=AB(10) + AB(12), writes=AB(12))
                S.dma("sp", "ysc", lambda e: e.dma_start(out=yscr[fcg * 128:(fcg + 1) * 128, :], in_=acc),
                      reads=AB(9), writes=[B_y[0]])

            for g in range(4):
                slab_g, sb_g = ws.get(wslabA(W[i, "pw1"], D + g * 512), "A")
                slab_a, sb_a = ws.get(wslabA(W[i, "pw1"], g * 512), "A", live=1)
                if S.dry:
                    continue
                for tb in (1, 0):
                    for nch in range(4):
                        glu_unit(g, nch, tb, slab_g, sb_g, slab_a, sb_a)
                    if tb == 1:
                        halo_send(g)
                for cp_ in range(2):
                    n0, n1 = 2 * cp_, 2 * cp_ + 1
                    diag_build(g, n0)
                    diag_build(g, n1)
                    conv_mm(g, n0, 1)
                    conv_mm(g, n1, 1)
                    if cp_ == 0:
                        halo_recv(g)
                    conv_mm(g, n0, 0)
                    conv_evac(g, n0)
                    conv_mm(g, n1, 0)
                    conv_evac(g, n1)
            mean, rstd = AF32(8), AF32(9)
            if DEBUG.get("conv_y"):
                if not S.dry:
                    tok = S.dma("sp", "dbg", lambda e: e.dma_start(out=outT[:, :], in_=yscr[:, :]), reads=B_y, writes=[B_out])
                    S.final.append((tok[0], tok[1]))
                return
            if not S.dry:
                ln_stats_finish(s1c, s2c, B_ar[11], B_ar[12], mean, rstd, B_ar[8], B_ar[9], AF32(10), B_ar[10])
            yb = AF32(0, 4).rearrange("p (k t) -> p k t", k=4)
            tt_ = [AF32(4), AF32(5)]
            hbvs = [(ABF(6, 2).rearrange("p (k t) -> p k t", k=4), AB(6, 2)), (ABF(10, 2).rearrange("p (k t) -> p k t", k=4), AB(10, 2))]

            def norm_chunk(g, nch):
                hbv, hbb = hbvs[1 if g == 1 else 0]
                fcg = g * 4 + nch
                t1 = tt_[nch % 2]
                tb_ = AB(4 + nch % 2)
                S.op("dve", lambda e: e.tensor_tensor(out=t1, in0=yb[:, nch, :], in1=mean, op=ALU.subtract),
                     reads=AB(0, 4) + AB(8), writes=tb_)
                S.op("dve", lambda e: e.tensor_tensor(out=t1, in0=t1, in1=rstd, op=ALU.mult), reads=tb_ + AB(9), writes=tb_)
                S.op("act", lambda e: e.activation(out=hbv[:, nch, :], in_=t1, func=AF.Silu,
                                                   scale=cpt[:, CP_LNG + fcg:CP_LNG + fcg + 1],
                                                   bias=cpt[:, CP_LNB + fcg:CP_LNB + fcg + 1]),
                     reads=tb_ + AB(13), writes=hbb)

            def yload(g):
                S.dma("sp", "yld", lambda e: e.dma_start(out=yb, in_=yscr[g * 512:(g + 1) * 512, :].rearrange("(k p) t -> p k t", p=128)),
                      reads=B_y, writes=AB(0, 4))

            for g in range(4):
                slab, sbuf_ = ws.get(wslabB(W[i, "pw2"], g * 512), "B")
                if S.dry:
                    continue
                yload(g)
                for nch in range(4):
                    norm_chunk(g, nch)
                hbv, hbb = hbvs[1 if g == 1 else 0]
                if g == 0:
                    proj_acc(hbv, hbb, 4, slab, sbuf_)
                    for n in range(16):
                        S.op("dve", lambda e, n=n: e.tensor_scalar(out=xr[:, n, :], in0=xr[:, n, :],
                                                                   scalar1=cpt[:, CP_BPW2 + n:CP_BPW2 + n + 1], scalar2=None, op0=ALU.add),
                             reads=[B_xr[n]] + AB(13), writes=[B_xr[n]])
                else:
                    proj_acc(hbv, hbb, 4, slab, sbuf_, last=(g == 3))

        epsc = sb("epsc", [128, 1], F32)
        onec = sb("onec", [128, 1], F32)

        def prologue():
            S.dma("sp", "pp", lambda e: e.dma_start(out=pp[:, :], in_=ppd[:, :]), reads=(), writes=[B_pp])
            S.dma("sp", "cst", lambda e: e.dma_start(out=ones_f[:, :], in_=cstd[:, CS_ONES:CS_ONES + 128]), reads=(), writes=[B_cst])
            S.dma("sp", "cst2", lambda e: e.dma_start(out=ident[:, :], in_=cstd[0:16, CS_ID:CS_ID + 16]), reads=(), writes=[B_cst])
            S.dma("pool", "cstb", lambda e: e.dma_start(out=ones_b[:, :], in_=cstd[:, CS_ONES:CS_ONES + 128]), reads=(), writes=[B_cst])
            S.dma("pool", "cstb2", lambda e: e.dma_start(out=tri_b[:, :], in_=cstd[:, CS_TRI:CS_TRI + 128]), reads=(), writes=[B_cst])
            S.dma("pool", "cstb3", lambda e: e.dma_start(out=ident_b[:, :], in_=cstd[:, CS_IDB:CS_IDB + 128]), reads=(), writes=[B_cst])
            S.op("pool", lambda e: e.memset(epsc[:, :], LN_EPS), reads=(), writes=[B_cst])
            S.op("pool", lambda e: e.memset(onec[:, :], 1.0), reads=(), writes=[B_cst])
            S.op("dve", lambda e: e.tensor_scalar(out=nbf[:, :], in0=pp[0:48, PP_BF:PP_BF + 2], scalar1=-1.0, scalar2=None, op0=ALU.mult),
                 reads=[B_pp], writes=[B_nbf])
            for fc in range(FC):
                S.dma("sp", f"xin{fc}", lambda e, fc=fc: e.dma_start(out=xr[:, fc, :], in_=xT[fc * 128:(fc + 1) * 128, :]),
                      reads=(), writes=[B_xr[fc]])
                S.op("act" if fc % 2 == 0 else "dve",
                     (lambda e, fc=fc: e.activation(out=xb[:, fc, :], in_=xr[:, fc, :], func=AF.Copy)) if fc % 2 == 0 else
                     (lambda e, fc=fc: e.tensor_copy(out=xb[:, fc, :], in_=xr[:, fc, :])),
                     reads=[B_xr[fc]], writes=[B_xb[fc]])

        def model():
            if not S.dry:
                prologue()
                state["after_ln"] = True
            last = layers[-1]
            for i in layers:
                kind = i % 3
                if do_mixer:
                    if kind == 0:
                        fox_layer(i)
                    elif kind == 1:
                        rel_layer(i)
                    else:
                        conv_layer(i)
                        if DEBUG.get("conv_y"):
                            break
                    if not S.dry:
                        layer_norm(PP_LN + i * 64, PP_LN + i * 64 + 16, final=(not do_ffn and i == last))
                if do_ffn:
                    ffn(i)
                    if not S.dry:
                        layer_norm(PP_LN + i * 64 + 32, PP_LN + i * 64 + 48, final=(i == last))

        S.dry = True
        model()
        S.dry = False
        state["pair"] = 0
        state["first_acc"] = True
        model()
        S.emit()
    return nc


def _cols(v):
    v = np.asarray(v, np.float32)
    return np.ascontiguousarray(v.reshape(-1, 128).T)


def _consts():
    c = np.zeros((128, NCS), np.float32)
    c[:, CS_ONES:CS_ONES + 128] = 1.0
    k = np.arange(128)[:, None]
    q = np.arange(128)[None, :]
    c[:, CS_TRI:CS_TRI + 128] = (q >= k).astype(np.float32)
    c[0:16, CS_ID:CS_ID + 16] = np.eye(16, dtype=np.float32)
    c[:, CS_IDB:CS_IDB + 128] = np.eye(128, dtype=np.float32)
    for h in range(16):
        c[h, CS_SEL + h * 128:CS_SEL + (h + 1) * 128] = 1.0
        c[32 + h, CS_SEL + h * 128:CS_SEL + (h + 1) * 128] = 1.0
    return c


def _relb_table(rel_bias):
    rb = np.asarray(rel_bias, np.float32)
    r = np.arange(5)[:, None, None]
    p = np.arange(128)[None, :, None]
    j = np.arange(128)[None, None, :]
    kpos = (r - 4) * 128 + p
    kch = np.floor_divide(kpos, 64)
    qch = j // 64
    valid = (kch >= qch - 8) & (kch <= qch)
    idx = np.clip(j - kpos, -128, 128) + 128
    idx, valid = np.broadcast_arrays(idx, valid)
    tab = rb[:, idx]
    tab = np.where(valid[None], tab, np.float32(NEG)).astype(np.float32)
    return np.ascontiguousarray(tab.transpose(0, 2, 1, 3).reshape(H, 128, 640))


def make_in_maps(inp, layers, do_mixer=True, do_ffn=True):
    x = np.asarray(inp["x"], np.float32)
    shared = {"cst": _consts()}
    pp = np.zeros((128, NPP), np.float32)
    for i in range(DEPTH):
        b = PP_LN + i * 64
        pp[:, b:b + 16] = _cols(inp["ln_mix_g"][i])
        pp[:, b + 16:b + 32] = _cols(inp["ln_mix_b"][i])
        pp[:, b + 32:b + 48] = _cols(inp["ln_ffn_g"][i])
        pp[:, b + 48:b + 64] = _cols(inp["ln_ffn_b"][i])
    for j in range(2):
        bf = np.asarray(inp["fox_b_f"][j], np.float32)
        pp[0:16, PP_BF + j] = bf
        pp[32:48, PP_BF + j] = bf
    for i in layers:
        kind, j = i % 3, i // 3
        if do_mixer:
            if kind == 0:
                shared[f"wqkv{i}"] = np.asarray(inp["fox_w_qkv"][j], np.float32)
                shared[f"wo{i}"] = np.asarray(inp["fox_w_o"][j], np.float32)
                wf = np.asarray(inp["fox_w_f"][j], np.float32)
                wfe = np.zeros((D, 48), np.float32)
                wfe[:, 0:16] = wf
                wfe[:, 32:48] = wf
                shared[f"wf{i}"] = np.ascontiguousarray(wfe.reshape(16, 128, 48).transpose(1, 0, 2).reshape(128, 16 * 48))
            elif kind == 1:
                shared[f"wqkv{i}"] = np.asarray(inp["rel_w_qkv"][j], np.float32)
                shared[f"wo{i}"] = np.asarray(inp["rel_w_o"][j], np.float32)
                shared[f"relb{i}"] = _relb_table(inp["rel_bias"][j])
            else:
                shared[f"pw1{i}"] = np.asarray(inp["conv_w_pw1"][j], np.float32)
                shared[f"pw2{i}"] = np.asarray(inp["conv_w_pw2"][j], np.float32)
                cp = np.zeros((128, NCP), np.float32)
                cp[:, CP_BPW1:CP_BPW1 + 32] = _cols(inp["conv_b_pw1"][j])
                wdw = np.asarray(inp["conv_w_dw"][j], np.float32)
                cp[:, CP_WDW:CP_WDW + 496] = wdw.reshape(31, 16, 128).transpose(2, 1, 0).reshape(128, 496)
                cp[:, CP_BDW:CP_BDW + 16] = _cols(inp["conv_b_dw"][j])
                cp[:, CP_LNG:CP_LNG + 16] = _cols(inp["conv_ln_g"][j])
                cp[:, CP_LNB:CP_LNB + 16] = _cols(inp["conv_ln_b"][j])
                cp[:, CP_BPW2:CP_BPW2 + 16] = _cols(inp["conv_b_pw2"][j])
                shared[f"cp{i}"] = cp
        if do_ffn:
            shared[f"wg{i}"] = np.asarray(inp["ffn_w_gate"][i], np.float32)
            shared[f"wu{i}"] = np.asarray(inp["ffn_w_up"][i], np.float32)
            shared[f"wd{i}"] = np.asarray(inp["ffn_w_down"][i], np.float32)
    maps = []
    for c in range(NCORES):
        b, hf = c // 2, c % 2
        m = dict(shared)
        m["xT"] = np.ascontiguousarray(x[b, hf * T:(hf + 1) * T, :].T)
        ppc_ = pp.copy()
        ppc_[:, PP_CTXMASK] = 0.0 if hf == 1 else NEG
        ppc_[:, PP_HASPREV] = 1.0 if hf == 1 else 0.0
        m["pp"] = ppc_
        maps.append(m)
    return maps


def run(inp, layers, do_mixer=True, do_ffn=True):
    nc = build(layers, do_mixer, do_ffn)
    maps = make_in_maps(inp, layers, do_mixer, do_ffn)
    res = run_bass_kernel_spmd(nc, maps, core_ids=list(range(NCORES)))
    out = np.empty((4, 2 * T, D), np.float32)
    for c in range(NCORES):
        b, hf = c // 2, c % 2
        out[b, hf * T:(hf + 1) * T, :] = np.asarray(res.results[c]["outT"]).T
    return out


def kernel(**inputs):
    return run(inputs, [0, 1, 2, 3])
```
